# Optimizing a Trainium2 kernel written in Bass

```python
import math
import jax, jax.numpy as jnp
from jax import lax
import numpy as np

D_MODEL = 1024
BATCH = 2
SEQ = 8192
DEPTH = 2

SSD_EXPAND = 2
D_INNER = SSD_EXPAND * D_MODEL
SSD_HEAD_DIM = 64
SSD_HEADS = D_INNER // SSD_HEAD_DIM
SSD_GROUPS = 8
SSD_STATE = 128
SSD_CONV = 5
SSD_CHUNK = 128
CONV_DIM = D_INNER + 2 * SSD_GROUPS * SSD_STATE
ATTN_HEAD_DIM = 64
ATTN_Q_HEADS = D_MODEL // ATTN_HEAD_DIM
ATTN_KV_HEADS = 4
ATTN_WIDTH = ATTN_Q_HEADS * ATTN_HEAD_DIM
KV_WIDTH = ATTN_KV_HEADS * ATTN_HEAD_DIM
WINDOW = 128
BLOCK = 128
N_BUCKETS = 32
MAX_DISTANCE = 128
D_FF = 4 * D_MODEL
EPS = 1e-6
IN_SPLITS = (D_INNER, CONV_DIM, 2 * SSD_HEADS, ATTN_WIDTH, KV_WIDTH, KV_WIDTH, 2 * D_MODEL)
N_IN = sum(IN_SPLITS)

kernel_name = 'hybrid_ssd_swa_encoder'


def rmsnorm(x, g):
    xf = x.astype(jnp.float32)
    y = xf * lax.rsqrt(jnp.mean(xf * xf, axis=-1, keepdims=True) + EPS)
    return (y * g.astype(jnp.float32)).astype(x.dtype)


def split_cols(t, sizes):
    idx, acc = [], 0
    for s in sizes[:-1]:
        acc += s
        idx.append(acc)
    return jnp.split(t, idx, axis=-1)


def depthwise_conv(u, w, b):
    pad = (SSD_CONV - 1) // 2
    y = lax.conv_general_dilated(u, w.astype(u.dtype), window_strides=(1,), padding=[(pad, pad)],
                                 dimension_numbers=('NWC', 'WIO', 'NWC'),
                                 feature_group_count=u.shape[-1])
    return y + b.astype(u.dtype)


def segsum_exp(a):
    cs = jnp.cumsum(a, axis=-1)
    diff = cs[..., :, None] - cs[..., None, :]
    L = a.shape[-1]
    mask = jnp.tril(jnp.ones((L, L), dtype=bool))
    return jnp.exp(jnp.where(mask, diff, -jnp.inf))


def ssd_scan(x, dt, a, b, c):
    Bsz, S, G, R, P = x.shape
    N = b.shape[-1]
    nc, L = S // SSD_CHUNK, SSD_CHUNK
    x = x.reshape(Bsz, nc, L, G, R, P)
    dt = dt.reshape(Bsz, nc, L, G, R)
    b = b.reshape(Bsz, nc, L, G, N)
    c = c.reshape(Bsz, nc, L, G, N)
    a_dt = jnp.moveaxis(dt * a, 2, -1)
    a_cs = jnp.cumsum(a_dt, axis=-1)
    xdt = x * dt[..., None]
    decay = segsum_exp(a_dt)
    cb = jnp.einsum('bclgn,bcsgn->bcgls', c, b)
    y_diag = jnp.einsum('bcgls,bcgrls,bcsgrp->bclgrp', cb, decay, xdt)
    decay_states = jnp.exp(a_cs[..., -1:] - a_cs)
    states = jnp.einsum('bclgn,bcgrl,bclgrp->bcgrpn', b, decay_states, xdt)
    chunk_decay = jnp.exp(a_cs[..., -1])

    def step(h, inp):
        st, dec = inp
        return h * dec[..., None, None] + st, h

    h0 = jnp.zeros((Bsz, G, R, P, N), x.dtype)
    _, prev = lax.scan(step, h0, (jnp.moveaxis(states, 1, 0), jnp.moveaxis(chunk_decay, 1, 0)))
    prev = jnp.moveaxis(prev, 0, 1)
    y_off = jnp.einsum('bclgn,bcgrpn,bcgrl->bclgrp', c, prev, jnp.exp(a_cs))
    return (y_diag + y_off).reshape(Bsz, S, G, R, P)


def gated_rmsnorm(y, z, w):
    u = y * jax.nn.silu(z.astype(jnp.float32))
    ug = u.reshape(u.shape[:-1] + (SSD_GROUPS, D_INNER // SSD_GROUPS))
    ug = ug * lax.rsqrt(jnp.mean(ug * ug, axis=-1, keepdims=True) + EPS)
    return ug.reshape(u.shape) * w.astype(jnp.float32)


def ssd_branch(z, xbc, dt_raw, conv_w, conv_b, dt_bias, a_log, d_skip, norm_w, w_out):
    Bsz, S = z.shape[:2]
    G, R, P, N = SSD_GROUPS, SSD_HEADS // SSD_GROUPS, SSD_HEAD_DIM, SSD_STATE
    xbc = jax.nn.silu(depthwise_conv(xbc, conv_w, conv_b))
    xs, bs, cs = split_cols(xbc, (D_INNER, G * N, G * N))
    xs = xs.astype(jnp.float32).reshape(Bsz, S, G, R, P)
    bs = bs.astype(jnp.float32).reshape(Bsz, S, G, N)
    cs = cs.astype(jnp.float32).reshape(Bsz, S, G, N)
    dt = jax.nn.softplus(dt_raw.astype(jnp.float32).reshape(Bsz, S, 2, G, R)
                         + dt_bias.astype(jnp.float32).reshape(2, G, R))
    a = -jnp.exp(a_log.astype(jnp.float32)).reshape(2, G, R)
    y_fwd = ssd_scan(xs, dt[:, :, 0], a[0], bs, cs)
    fl = lambda t: jnp.flip(t, axis=1)
    y_bwd = fl(ssd_scan(fl(xs), fl(dt[:, :, 1]), a[1], fl(bs), fl(cs)))
    y = y_fwd + y_bwd + xs * d_skip.astype(jnp.float32).reshape(G, R, 1)
    y = gated_rmsnorm(y.reshape(Bsz, S, D_INNER), z, norm_w)
    return y.astype(z.dtype) @ w_out


def t5_bucket(rel):
    nb = N_BUCKETS // 2
    max_exact = nb // 2
    ret = jnp.where(rel > 0, nb, 0)
    n = jnp.abs(rel)
    nf = jnp.maximum(n, 1).astype(jnp.float32)
    large = max_exact + (jnp.log(nf / max_exact) / math.log(MAX_DISTANCE / max_exact)
                         * (nb - max_exact)).astype(jnp.int32)
    large = jnp.minimum(large, nb - 1)
    return ret + jnp.where(n < max_exact, n, large)


def window_attention(q, k, v, sink, rel_table):
    Bsz, S = q.shape[:2]
    nb = S // BLOCK
    rep = ATTN_Q_HEADS // ATTN_KV_HEADS
    qb = q.reshape(Bsz, nb, BLOCK, ATTN_KV_HEADS, rep, ATTN_HEAD_DIM)

    def band(t):
        t = t.reshape(Bsz, S, ATTN_KV_HEADS, ATTN_HEAD_DIM)
        t = jnp.pad(t, ((0, 0), (BLOCK, BLOCK), (0, 0), (0, 0)))
        t = t.reshape(Bsz, nb + 2, BLOCK, ATTN_KV_HEADS, ATTN_HEAD_DIM)
        return jnp.concatenate([t[:, :-2], t[:, 1:-1], t[:, 2:]], axis=2)

    kb, vb = band(k), band(v)
    logits = jnp.einsum('bnqgrd,bnkgd->bngrqk', qb, kb).astype(jnp.float32) * (ATTN_HEAD_DIM ** -0.5)
    i = jnp.arange(BLOCK)[:, None]
    j = jnp.arange(3 * BLOCK)[None, :]
    rel = j - BLOCK - i
    bias = rel_table[t5_bucket(rel)].astype(jnp.float32)
    bias = jnp.transpose(bias, (2, 0, 1)).reshape(ATTN_KV_HEADS, rep, BLOCK, 3 * BLOCK)
    kpos = jnp.arange(nb)[:, None] * BLOCK + j - BLOCK
    valid = (jnp.abs(rel) <= WINDOW)[None] & ((kpos >= 0) & (kpos < S))[:, None, :]
    logits = jnp.where(valid[None, :, None, None], logits + bias, -jnp.inf)
    sink_l = sink.astype(jnp.float32).reshape(1, 1, ATTN_KV_HEADS, rep, 1, 1)
    m = jnp.maximum(jnp.max(logits, axis=-1, keepdims=True), sink_l)
    p = jnp.exp(logits - m)
    p = p / (jnp.sum(p, axis=-1, keepdims=True) + jnp.exp(sink_l - m))
    out = jnp.einsum('bngrqk,bnkgd->bnqgrd', p.astype(v.dtype), vb)
    return out.reshape(Bsz, S, ATTN_WIDTH)


def setup_inputs(seed: int = 0) -> dict:
    key = jax.random.key(seed)
    ks = jax.random.split(key, 24)
    f32 = jnp.float32
    nrm = lambda k, shape, s: jax.random.normal(k, shape, f32) * s
    gain = lambda k, shape: 1.0 + 0.05 * jax.random.normal(k, shape, f32)
    dt0 = jnp.exp(jax.random.uniform(ks[10], (DEPTH, 2, SSD_HEADS), f32, math.log(1e-3), math.log(1e-1)))
    return {
        'x': jax.random.normal(ks[0], (BATCH, SEQ, D_MODEL), f32),
        'pre_mix_norm': gain(ks[1], (DEPTH, D_MODEL)),
        'w_in': nrm(ks[2], (DEPTH, D_MODEL, N_IN), D_MODEL ** -0.5),
        'b_gate': nrm(ks[3], (DEPTH, 2 * D_MODEL), 0.1),
        'conv_w': nrm(ks[4], (DEPTH, SSD_CONV, 1, CONV_DIM), SSD_CONV ** -0.5),
        'conv_b': nrm(ks[5], (DEPTH, CONV_DIM), 0.02),
        'dt_bias': dt0 + jnp.log(-jnp.expm1(-dt0)),
        'a_log': jnp.log(jax.random.uniform(ks[6], (DEPTH, 2, SSD_HEADS), f32, 1.0, 16.0)),
        'd_skip': gain(ks[7], (DEPTH, SSD_HEADS)),
        'ssd_norm': gain(ks[8], (DEPTH, D_INNER)),
        'w_ssd_out': nrm(ks[9], (DEPTH, D_INNER, D_MODEL), D_INNER ** -0.5),
        'attn_sink': nrm(ks[11], (DEPTH, ATTN_Q_HEADS), 0.5),
        'rel_bias_table': nrm(ks[12], (N_BUCKETS, ATTN_Q_HEADS), 0.5),
        'w_attn_out': nrm(ks[13], (DEPTH, ATTN_WIDTH, D_MODEL), ATTN_WIDTH ** -0.5),
        'w_o': nrm(ks[14], (DEPTH, D_MODEL, D_MODEL), D_MODEL ** -0.5),
        'post_mix_norm': gain(ks[15], (DEPTH, D_MODEL)),
        'pre_mlp_norm': gain(ks[16], (DEPTH, D_MODEL)),
        'w_mlp_in': nrm(ks[17], (DEPTH, D_MODEL, D_FF), D_MODEL ** -0.5),
        'w_mlp_out': nrm(ks[18], (DEPTH, D_FF, D_MODEL), D_FF ** -0.5),
        'post_mlp_norm': gain(ks[19], (DEPTH, D_MODEL)),
    }


def reference(x, pre_mix_norm, w_in, b_gate, conv_w, conv_b, dt_bias, a_log, d_skip, ssd_norm,
              w_ssd_out, attn_sink, rel_bias_table, w_attn_out, w_o, post_mix_norm,
              pre_mlp_norm, w_mlp_in, w_mlp_out, post_mlp_norm):
    for l in range(DEPTH):
        h = rmsnorm(x, pre_mix_norm[l])
        proj = h @ w_in[l]
        z, xbc, dt_raw, q, k, v, gates = split_cols(proj, IN_SPLITS)
        y_ssd = ssd_branch(z, xbc, dt_raw, conv_w[l], conv_b[l], dt_bias[l], a_log[l],
                           d_skip[l], ssd_norm[l], w_ssd_out[l])
        y_attn = window_attention(q, k, v, attn_sink[l], rel_bias_table) @ w_attn_out[l]
        g = jax.nn.sigmoid((gates + b_gate[l]).astype(jnp.float32)).astype(x.dtype)
        g_ssd, g_attn = g[..., :D_MODEL], g[..., D_MODEL:]
        mixed = (g_ssd * y_ssd + g_attn * y_attn) @ w_o[l]
        x = x + rmsnorm(mixed, post_mix_norm[l])
        h = rmsnorm(x, pre_mlp_norm[l])
        f = jnp.square(jax.nn.relu(h @ w_mlp_in[l])) @ w_mlp_out[l]
        x = x + rmsnorm(f, post_mlp_norm[l])
    return x
```

```python
import math
from contextlib import ExitStack
import numpy as np
import concourse.bass as bass
import concourse.mybir as mybir
from concourse.bass_utils import run_bass_kernel_spmd

F32 = mybir.dt.float32
BF16 = mybir.dt.bfloat16
AF = mybir.ActivationFunctionType
ALU = mybir.AluOpType
AX = mybir.AxisListType
PE, ACT, DVE, POOL, SP = range(5)
SAME_ENGINE_SYNC = True
DEBUG = False

D = 1024
NIN = 9792
OFF_Z, OFF_X, OFF_B, OFF_C, OFF_DT, OFF_Q, OFF_K, OFF_V, OFF_G = 0, 2048, 4096, 5120, 6144, 6208, 7232, 7488, 7744
NEG = -30000.0
EPS = 1e-6
NCH = 16


class Buf:
    __slots__ = ("name", "wr", "rd", "dsem", "dcnt")

    def __init__(self, name=""):
        self.name = name
        self.wr = None
        self.rd = {}
        self.dsem = None
        self.dcnt = 0


class FW:
    def __init__(self, nc):
        self.nc = nc
        self.q = [[] for _ in range(5)]
        self.esem = [nc.alloc_semaphore(name=f"es{i}") for i in range(5)]
        self.ecnt = [0] * 5
        self.waited = [dict() for _ in range(5)]
        self.nsem = 5

    def _collect(self, e, reads, writes):
        waits = {}
        wt = self.waited[e]

        def need(ev, same_ok):
            if ev is None:
                return
            sem, val, src = ev
            if src == e and same_ok:
                return
            k = id(sem)
            if wt.get(k, 0) >= val:
                return
            if k not in waits or waits[k][1] < val:
                waits[k] = (sem, val)

        raw_same_ok = (e == PE) or (not SAME_ENGINE_SYNC)
        for b in reads:
            need(b.wr, raw_same_ok)
        for b in writes:
            need(b.wr, raw_same_ok)
            for ev in b.rd.values():
                need(ev, True)
        for k, (sem, val) in waits.items():
            wt[k] = val
        return list(waits.values())

    def op(self, e, fn, reads=(), writes=(), inc=True):
        wl = self._collect(e, reads, writes)
        val = self.ecnt[e] + 1
        if inc:
            self.ecnt[e] += 1
        sem_e = self.esem[e]
        ev = (sem_e, val, e)
        for b in reads:
            b.rd[e] = ev
        for b in writes:
            b.wr = ev
            b.rd = {}

        def emit(eng):
            for sem, v in wl:
                eng.wait_ge(sem, v)
            ins = fn(eng)
            if inc:
                ins.then_inc(sem_e, 1)

        self.q[e].append(emit)

    def dma(self, e, out, in_, reads=(), writes=(), owner=None):
        wl = self._collect(e, reads, writes)
        if owner is None:
            owner = writes[0] if writes else reads[0]
        if owner.dsem is None:
            owner.dsem = self.nc.alloc_semaphore(name=f"ds{self.nsem}")
            self.nsem += 1
        owner.dcnt += 16
        sem = owner.dsem
        ev = (sem, owner.dcnt, -1)
        for b in reads:
            b.rd[("d", id(sem))] = ev
        for b in writes:
            b.wr = ev
            b.rd = {}

        def emit(eng):
            for s_, v in wl:
                eng.wait_ge(s_, v)
            eng.dma_start(out=out, in_=in_).then_inc(sem, 16)

        self.q[e].append(emit)

    def finish(self, final_bufs):
        nc = self.nc
        fin = []
        for b in final_bufs:
            for ev in [b.wr] + list(b.rd.values()):
                if ev is not None:
                    fin.append((ev[0], ev[1]))
        q = self.q
        with nc.Block() as block:
            @block.tensor
            def _(eng):
                for f in q[PE]:
                    f(eng)

            @block.scalar
            def _(eng):
                for f in q[ACT]:
                    f(eng)

            @block.vector
            def _(eng):
                for f in q[DVE]:
                    f(eng)

            @block.gpsimd
            def _(eng):
                for f in q[POOL]:
                    f(eng)

            @block.sync
            def _(eng):
                for f in q[SP]:
                    f(eng)
                for sem, v in fin:
                    eng.wait_ge(sem, v)


def build(phase):
    nc = bass.Bass("TRN2", target_bir_lowering=False)
    fw = FW(nc)

    def din(name, shape, dt=F32):
        return nc.dram_tensor(name, shape, dt, kind="ExternalInput").ap()

    def dout(name, shape, dt=F32):
        return nc.dram_tensor(name, shape, dt, kind="ExternalOutput").ap()

    xh = din("xh", [18 * 128, D])
    w_in = din("w_in", [D, NIN])
    cwd = din("cw", [128, 32 * 5])
    cbd = din("cb", [128, 32])
    g1d = din("g1", [128, 8])
    rowp = din("rowp", [1, 2224])
    cst = din("cst", [128, 7 * 128])
    if phase == 2:
        w_so = din("w_so", [2048, D])
        w_ao = din("w_ao", [D, D])
        w_o = din("w_o", [D, D])
        w1 = din("w1", [D, 4096])
        w2 = din("w2", [4096, D])
        g3d = din("g3", [128, 8])
        nwd = din("nw", [128, 16])
        bgd = din("bg", [128, 16])
        ohd = din("oh", [33, 512])
        tbd = din("tb", [33, 16])
        mkd = din("mk", [128, 16])
        sf4 = din("sf4", [4 * 128, 2048])
        sb4 = din("sb4", [4 * 128, 2048])
        dec4 = din("dec4", [128, 4 * 64])
        hbl_in = din("hbl", [NCH * 128, 2048], BF16)
        yout = dout("y", [2048, D])
        dbg_b = Buf("dbg")
        if DEBUG:
            x1d = dout("x1d", [2048, D])
            dbg_u = dout("dbg_u", [2048, 2048])
            dbg_o = dout("dbg_o", [NCH * 64, 16 * 128], BF16)
            dbg_m = dout("dbg_m", [NCH * 128, 8 * 128], BF16)
        else:
            x1d = nc.dram_tensor("x1d", [2048, D], F32).ap()
        vreld = nc.dram_tensor("vreld", [16, 512], F32)
    else:
        hf_out = dout("hf", [128, 2048])
        hb_out = dout("hb", [128, 2048])
        dec_out = dout("dec", [128, 64])
        hbl_out = dout("hbl", [NCH * 128, 2048], BF16)

    cnt = [0]

    scope = [None]

    def sb(shape, dt, name=None):
        cnt[0] += 1
        nm = f"s_{name or 't'}{cnt[0]}"
        if scope[0] is not None:
            return scope[0].enter_context(nc.sbuf_tensor(nm, shape, dt)), Buf(nm)
        return nc.alloc_sbuf_tensor(nm, shape, dt), Buf(nm)

    def mm(out, lhsT, rhs, start, stop, reads, writes, inc=None):
        fw.op(PE, lambda e: e.matmul(out, lhsT=lhsT, rhs=rhs, start=start, stop=stop), reads, writes,
              inc=stop if inc is None else inc)

    def tr(out, in_, ident, reads, writes):
        fw.op(PE, lambda e: e.transpose(out=out, in_=in_, identity=ident), reads, writes)

    def act(out, in_, func, reads, writes, **kw):
        fw.op(ACT, lambda e: e.activation(out=out, in_=in_, func=func, **kw), reads, writes)

    def tt(eng, out, in0, in1, op, reads, writes):
        fw.op(eng, lambda e: e.tensor_tensor(out=out, in0=in0, in1=in1, op=op), reads, writes)

    def ts(eng, out, in0, s1, s2, op0, op1, reads, writes):
        if s2 is None:
            fw.op(eng, lambda e: e.tensor_scalar(out=out, in0=in0, scalar1=s1, scalar2=None, op0=op0), reads, writes)
        else:
            fw.op(eng, lambda e: e.tensor_scalar(out=out, in0=in0, scalar1=s1, scalar2=s2, op0=op0, op1=op1), reads, writes)

    def stt(eng, out, in0, scalar, in1, op0, op1, reads, writes):
        fw.op(eng, lambda e: e.scalar_tensor_tensor(out=out, in0=in0, scalar=scalar, in1=in1, op0=op0, op1=op1),
              reads, writes)

    def cp(eng, out, in_, reads, writes):
        if eng == ACT:
            fw.op(ACT, lambda e: e.copy(out=out, in_=in_), reads, writes)
        else:
            fw.op(eng, lambda e: e.tensor_copy(out=out, in_=in_), reads, writes)

    def recip(out, in_, reads, writes):
        fw.op(DVE, lambda e: e.reciprocal(out=out, in_=in_), reads, writes)

    pst = [nc.alloc_psum_tensor(f"ps{i}", [128, 512], F32) for i in range(8)]
    psb = [Buf(f"ps{i}") for i in range(8)]
    psi = [0]

    def psum():
        i = psi[0] % 8
        psi[0] += 1
        return pst[i], psb[i]

    NW = 2
    wpool = [sb([128, 4096], BF16, "w") for _ in range(NW)]
    wi = [0]

    def load_w(src, k, c):
        t, b = wpool[wi[0] % NW]
        wi[0] += 1
        view = t[:, 0:k * c].rearrange("p (k c) -> p k c", k=k)
        fw.dma(POOL, view, src.rearrange("(k p) c -> p k c", p=128), writes=[b])
        return view, b

    cs_t, cs_b = sb([128, 7 * 128], F32, "cst")
    fw.dma(SP, cs_t[:], cst, writes=[cs_b])
    ident32 = cs_t[:, 0:128]
    antiid = cs_t[:, 128:256]
    U_incl = cs_t[:, 256:384]
    U_rev = cs_t[:, 384:512]
    maskF = cs_t[:, 512:640]
    maskB = cs_t[:, 640:768]
    ones32 = cs_t[:, 768:896]
    idb_t, idb_b = sb([128, 128], BF16, "idb")
    fw.dma(POOL, idb_t[:], cst[:, 0:128], writes=[idb_b])
    identb = idb_t[:]
    rp_t, rp_b = sb([128, 2224], F32, "rowp")
    fw.dma(SP, rp_t[:], rowp.partition_broadcast(128).rearrange("p o c -> p (o c)"), writes=[rp_b])
    dtb = rp_t[:, 0:64]
    alog = rp_t[:, 64:128]
    dsk = rp_t[:, 128:160]
    sink = rp_t[:, 160:176]
    pmix = rp_t[:, 176:1200]
    pmlp = rp_t[:, 1200:2224]
    cw_t, cw_b = sb([128, 160], F32, "cw")
    fw.dma(SP, cw_t[:], cwd, writes=[cw_b])
    cb_t, cb_b = sb([128, 32], F32, "cb")
    fw.dma(SP, cb_t[:], cbd, writes=[cb_b])
    g1_t, g1_b = sb([128, 8], F32, "g1")
    fw.dma(SP, g1_t[:], g1d, writes=[g1_b])
    a_t, a_b = sb([128, 64], F32, "a")
    act(a_t[:], alog, AF.Exp, [rp_b], [a_b])
    ts(DVE, a_t[:], a_t[:], -1.0, None, ALU.mult, ALU.bypass, [a_b], [a_b])

    junk_t, junk_b = sb([128, 1024], BF16, "junk")

    sspool = [sb([128, 8], F32, "ss") for _ in range(4)]
    ssi = [0]

    def rms_rstd(src, src_b, n, width, eps=EPS):
        ss_full, ss_b = sspool[ssi[0] % 4]
        ssi[0] += 1
        ss_t = ss_full[:, 0:n]
        for g in range(n):
            act(junk_t[:, 0:width], src[:, g * width:(g + 1) * width], AF.Square, [src_b], [junk_b, ss_b],
                accum_out=ss_t[:, g:g + 1])
        act(ss_t, ss_t, AF.Sqrt, [ss_b], [ss_b], bias=eps, scale=1.0 / width)
        recip(ss_t, ss_t, [ss_b], [ss_b])
        return ss_t, ss_b

    hTd = nc.dram_tensor("hTd", [128, 8, 2304], BF16).ap()
    hTd_b = Buf("hTd")
    htile = [sb([128, 8, 128], BF16, "htile") for _ in range(2)]
    xin = [sb([128, D], F32, "xin") for _ in range(2)]
    hb2 = [sb([128, D], BF16, "hb") for _ in range(1)]

    def norm_transpose(src, src_b, gain, gain_b, dst, dst_b):
        r_t, r_b = rms_rstd(src, src_b, 1, D)
        h_t, h_b = hb2[0]
        act(h_t[:], src, AF.Copy, [src_b, r_b], [h_b], scale=r_t[:, 0:1])
        p_t, p_b = psum()
        pv = p_t[:].bitcast(BF16)
        for k in range(8):
            tr(pv[:, k * 128:(k + 1) * 128], h_t[:, k * 128:(k + 1) * 128], identb, [h_b, idb_b], [p_b])
        tt(DVE, dst, pv.rearrange("p (k c) -> p k c", k=8), gain.unsqueeze(2).to_broadcast([128, 8, 128]),
           ALU.mult, [p_b, gain_b], [dst_b])

    def small(n=64):
        return sb([128, NCH, n], F32, "dtq")

    dt_t, dt_b = small()
    adt_t, adt_b = small()
    cs2_t, cs2_b = small()
    tot_t, tot_b = small()
    nbq = [sb([128, 64], F32, "nb") for _ in range(2)]
    eeq = [sb([128, 64], F32, "ee") for _ in range(2)]
    etq = [sb([128, 64], F32, "et") for _ in range(2)]

    def chunk_small(c):
        nb_t, nb_b = nbq[c % 2]
        ee_t, ee_b = eeq[c % 2]
        et_t, et_b = etq[c % 2]
        act(nb_t[:], dt_t[:, c, :], AF.Ln, [dt_b], [nb_b])
        tt(DVE, nb_t[:], nb_t[:], cs2_t[:, c, :], ALU.subtract, [nb_b, cs2_b], [nb_b])
        act(ee_t[:], cs2_t[:, c, :], AF.Exp, [cs2_b], [ee_b])
        act(et_t[:], tot_t[:, c, :], AF.Exp, [tot_b], [et_b])
        return nb_t, nb_b, ee_t, ee_b, et_t, et_b

    wdt_v, wdt_b = load_w(w_in[:, OFF_DT:OFF_DT + 64], 8, 64)
    for t in range(18):
        x_t, x_b = xin[t % 2]
        fw.dma(SP, x_t[:], xh[t * 128:(t + 1) * 128, :], writes=[x_b])
        ht_t, ht_b = htile[t % 2]
        norm_transpose(x_t[:], x_b, g1_t[:], g1_b, ht_t[:], ht_b)
        fw.dma(SP, hTd[:, :, t * 128:(t + 1) * 128], ht_t[:], reads=[ht_b], writes=[hTd_b], owner=ht_b)
        if 1 <= t <= 16:
            c = t - 1
            p_t, p_b = psum()
            for k in range(8):
                mm(p_t[:, 0:64], ht_t[:, k, :], wdt_v[:, k, :], k == 0, k == 7, [ht_b, wdt_b], [p_b])
            tt(DVE, dt_t[:, c, :], p_t[:, 0:64], dtb, ALU.add, [p_b, rp_b], [dt_b])
            act(dt_t[:, c, :], dt_t[:, c, :], AF.Exp, [dt_b], [dt_b])
            act(dt_t[:, c, :], dt_t[:, c, :], AF.Ln, [dt_b], [dt_b], bias=1.0)
            tt(DVE, adt_t[:, c, :], dt_t[:, c, :], a_t[:], ALU.mult, [dt_b, a_b], [adt_b])
            p2_t, p2_b = psum()
            mm(p2_t[:, 0:32], U_incl, adt_t[:, c, 0:32], True, True, [cs_b, adt_b], [p2_b])
            mm(p2_t[:, 32:64], U_rev, adt_t[:, c, 32:64], True, True, [cs_b, adt_b], [p2_b])
            mm(p2_t[:, 64:128], ones32, adt_t[:, c, :], True, True, [cs_b, adt_b], [p2_b])
            cp(DVE, cs2_t[:, c, :], p2_t[:, 0:64], [p2_b], [cs2_b])
            cp(DVE, tot_t[:, c, :], p2_t[:, 64:128], [p2_b], [tot_b])

    hwin = [sb([128, 8, 384], BF16, "hwin") for _ in range(2)]
    col = 128

    def load_window(c):
        hw_t, hw_b = hwin[c % 2]
        fw.dma(SP, hw_t[:], hTd[:, :, c * 128:c * 128 + 384], reads=[hTd_b], writes=[hw_b])
        return hw_t, hw_b

    xtok = [sb([128, 2048], BF16, "xtok") for _ in range(1)]
    btok = [sb([128, 1024], BF16, "btok") for _ in range(1)]
    bT = [sb([128, 8, 128], BF16, "bT") for _ in range(1)]
    cT = [sb([128, 8, 128], BF16, "cT") for _ in range(1)]
    usb = [sb([128, 132], F32, "usb") for _ in range(2)]
    cacc = [sb([128, 128], F32, "cacc") for _ in range(2)]
    fT = [sb([128, 4, 128], BF16, "fT") for _ in range(2)]

    def xbc_chunk(c, need_c, hT_t, hw_b):
        x_t, x_b = xtok[0]
        bk_t, bk_b = btok[0]
        bT_t, bT_b = bT[0]
        cT_t, cT_b = cT[0]
        hbs = [hw_b]
        nblk = 8 if need_c else 6
        for blk in range(nblk):
            wv, wb = load_w(w_in[:, OFF_X + blk * 512:OFF_X + (blk + 1) * 512], 8, 512)
            f_t, f_b = fT[blk % 2]
            for j in range(4):
                i = blk * 4 + j
                p_t, p_b = psum()
                for k in range(8):
                    mm(p_t[:, 0:132], wv[:, k, j * 128:(j + 1) * 128], hT_t[:, k, col - 2:col + 130],
                       k == 0, k == 7, hbs + [wb], [p_b])
                u_t, u_b = usb[i % 2]
                cp(ACT, u_t[:], p_t[:, 0:132], [p_b], [u_b])
                a_t2, a_b2 = cacc[i % 2]
                eng = DVE
                ts(eng, a_t2[:], u_t[:, 0:128], cw_t[:, i * 5:i * 5 + 1], cb_t[:, i:i + 1], ALU.mult, ALU.add,
                   [u_b, cw_b, cb_b], [a_b2])
                for kk in range(1, 5):
                    stt(eng, a_t2[:], u_t[:, kk:kk + 128], cw_t[:, i * 5 + kk:i * 5 + kk + 1], a_t2[:],
                        ALU.mult, ALU.add, [u_b, cw_b, a_b2], [a_b2])
                if i < 16:
                    act(f_t[:, j, :], a_t2[:], AF.Silu, [a_b2], [f_b])
                elif i < 24:
                    act(bT_t[:, i - 16, :], a_t2[:], AF.Silu, [a_b2], [bT_b])
                else:
                    act(cT_t[:, i - 24, :], a_t2[:], AF.Silu, [a_b2], [cT_b])
            if blk < 6:
                p_t, p_b = psum()
                pv = p_t[:].bitcast(BF16)
                for j in range(4):
                    src = f_t[:, j, :] if blk < 4 else bT_t[:, (blk - 4) * 4 + j, :]
                    srcb = f_b if blk < 4 else bT_b
                    tr(pv[:, j * 128:(j + 1) * 128], src, identb, [srcb, idb_b], [p_b])
                if blk < 4:
                    cp(ACT, x_t[:, blk * 512:(blk + 1) * 512], pv[:, 0:512], [p_b], [x_b])
                else:
                    cp(ACT, bk_t[:, (blk - 4) * 512:(blk - 3) * 512], pv[:, 0:512], [p_b], [bk_b])
        return x_t, x_b, bk_t, bk_b, bT_t, bT_b, cT_t, cT_b

    def bc32(ap2d):
        return ap2d.unsqueeze(2).to_broadcast([128, 32, 64])

    xw = [sb([128, 2048 if phase == 1 else 512], BF16, "xw") for _ in range(2)]
    finals = []

    if phase == 1:
        hf_t, hf_b = sb([128, 2048], F32, "hf")
        hb_t, hb_b = sb([128, 2048], F32, "hbs")
        hbb = [sb([128, 2048], BF16, "hbb") for _ in range(2)]
        suf_t, suf_b = sb([128, 32], F32, "suf")
        wf_t, wf_b = sb([128, 64], F32, "wf")
        fw.op(DVE, lambda e: e.memset(suf_t[:], 0.0), [], [suf_b])
        fw.op(DVE, lambda e: e.memset(hb_t[:], 0.0), [], [hb_b])
        fw.op(DVE, lambda e: e.memset(hf_t[:], 0.0), [], [hf_b])
        hbl_b = Buf("hblout")
        for c in range(NCH - 1, -1, -1):
            x_t, x_b, bk_t, bk_b, _, _, _, _ = xbc_chunk(c, False, *load_window(c))
            nb_t, nb_b, ee_t, ee_b, et_t, etot_b = chunk_small(c)
            tt(DVE, wf_t[:, 0:32], tot_t[:, c, 0:32], suf_t[:], ALU.add, [tot_b, suf_b], [wf_b])
            cp(DVE, wf_t[:, 32:64], tot_t[:, c, 32:64], [tot_b], [wf_b])
            tt(DVE, wf_t[:], wf_t[:], cs2_t[:, c, :], ALU.subtract, [wf_b, cs2_b], [wf_b])
            act(wf_t[:], wf_t[:], AF.Exp, [wf_b], [wf_b])
            tt(DVE, wf_t[:], wf_t[:], dt_t[:, c, :], ALU.mult, [wf_b, dt_b], [wf_b])
            tt(DVE, suf_t[:], suf_t[:], tot_t[:, c, 0:32], ALU.add, [suf_b, tot_b], [suf_b])
            hq_t, hq_b = hbb[c % 2]
            cp(ACT, hq_t[:], hb_t[:], [hb_b], [hq_b])
            fw.dma(SP, hbl_out[c * 128:(c + 1) * 128, :], hq_t[:], reads=[hq_b], writes=[hbl_b], owner=hq_b)
            tt(DVE, hb_t[:].rearrange("p (h d) -> p h d", h=32), hb_t[:].rearrange("p (h d) -> p h d", h=32),
               bc32(et_t[:, 32:64]), ALU.mult, [hb_b, etot_b], [hb_b])
            for d in range(2):
                xw_t, xw_b = xw[d]
                eng = DVE if d == 0 else POOL
                tt(eng, xw_t[:].rearrange("p (h d) -> p h d", h=32), x_t[:].rearrange("p (h d) -> p h d", h=32),
                   bc32(wf_t[:, d * 32:(d + 1) * 32]), ALU.mult, [x_b, wf_b], [xw_b])
                acc_t, acc_b = (hf_t, hf_b) if d == 0 else (hb_t, hb_b)
                for qd in range(4):
                    p_t, p_b = psum()
                    for j in range(2):
                        g = qd * 2 + j
                        mm(p_t[:, j * 256:(j + 1) * 256], bk_t[:, g * 128:(g + 1) * 128],
                           xw_t[:, g * 256:(g + 1) * 256], True, True, [bk_b, xw_b], [p_b])
                    tt(DVE, acc_t[:, qd * 512:(qd + 1) * 512], acc_t[:, qd * 512:(qd + 1) * 512], p_t[:],
                       ALU.add, [acc_b, p_b], [acc_b])
        ob = [Buf("o1"), Buf("o2"), Buf("o3")]
        fw.dma(SP, hf_out, hf_t[:], reads=[hf_b], writes=[ob[0]], owner=ob[0])
        fw.dma(SP, hb_out, hb_t[:], reads=[hb_b], writes=[ob[1]], owner=ob[1])
        td_t, td_b = sb([128, 64], F32, "td")
        cp(DVE, td_t[:], tot_t[:, 0, :], [tot_b], [td_b])
        for c in range(1, NCH):
            tt(DVE, td_t[:], td_t[:], tot_t[:, c, :], ALU.add, [td_b, tot_b], [td_b])
        fw.dma(SP, dec_out, td_t[:], reads=[td_b], writes=[ob[2]], owner=ob[2])
        finals = ob + [hbl_b] + [b for _, b in hbb]
        endt, endb = sb([128, 1], F32, "end")
        fw.op(POOL, lambda e: e.memset(endt[:], 0.0), [], [endb])
        fw.op(ACT, lambda e: e.copy(out=endt[:], in_=endt[:]), [endb], [endb])
        fw.finish(finals + [endb])
        return nc

    g3_t, g3_b = sb([128, 8], F32, "g3")
    fw.dma(SP, g3_t[:], g3d, writes=[g3_b])
    nw_t, nw_b = sb([128, 16], F32, "nw")
    fw.dma(SP, nw_t[:], nwd, writes=[nw_b])
    bg_t, bg_b = sb([128, 16], F32, "bg")
    fw.dma(SP, bg_t[:], bgd, writes=[bg_b])
    mk_t, mk_b = sb([128, 16], F32, "mk")
    fw.dma(SP, mk_t[:], mkd, writes=[mk_b])

    oh_t, oh_b = sb([33, 512], F32, "oh")
    fw.dma(SP, oh_t[:], ohd, writes=[oh_b])
    tb_t, tb_b = sb([33, 16], F32, "tb")
    fw.dma(SP, tb_t[:], tbd, writes=[tb_b])
    p_t, p_b = psum()
    mm(p_t[0:16, :], tb_t[:], oh_t[:], True, True, [tb_b, oh_b], [p_b])
    vr_t, vr_b = sb([16, 512], F32, "vr")
    cp(DVE, vr_t[:], p_t[0:16, :], [p_b], [vr_b])
    vrd_b = Buf("vreld")
    fw.dma(SP, vreld.ap(), vr_t[:], reads=[vr_b], writes=[vrd_b])
    bias_t, bias_b = sb([128, 16, 384], BF16, "bias")
    hk = [sb([128, 384], F32, "hk") for _ in range(1)]
    for h in range(16):
        hk_t, hk_b = hk[0]
        fw.dma(SP, hk_t[:], bass.AP(vreld, h * 512, [[1, 128], [1, 384]]), reads=[vrd_b], writes=[hk_b])
        p_t, p_b = psum()
        mm(p_t[:, 0:384], antiid, hk_t[:], True, True, [cs_b, hk_b], [p_b])
        cp(ACT, bias_t[:, h, :], p_t[:, 0:384], [p_b], [bias_b])

    hin = []
    u_t, u_b = sb([128, 2048], F32, "u")
    st_t, st_b = u_t, u_b
    dc_t, dc_b = sb([128, 256], F32, "dc4")
    fw.dma(SP, dc_t[:], dec4, writes=[dc_b])
    act(dc_t[:], dc_t[:], AF.Exp, [dc_b], [dc_b])
    dd_t, dd_b = sb([128, 32], F32, "dd")
    for d in range(2):
        h_t, h_b = sb([128, 2048], BF16, "hin")
        fw.op(DVE, lambda e, h_t=h_t: e.memset(h_t[:], 0.0), [], [h_b])
        order = range(4) if d == 0 else range(3, -1, -1)
        src = sf4 if d == 0 else sb4
        for r in order:
            mcol = mk_t[:, 2 + d * 4 + r:3 + d * 4 + r]
            fw.dma(SP, st_t[:], src[r * 128:(r + 1) * 128, :], writes=[st_b])
            ts(DVE, dd_t[:], dc_t[:, r * 64 + d * 32:r * 64 + d * 32 + 32], -1.0, mcol, ALU.add, ALU.mult,
               [dc_b, mk_b], [dd_b])
            ts(DVE, dd_t[:], dd_t[:], 1.0, None, ALU.add, ALU.bypass, [dd_b], [dd_b])
            hv = h_t[:].rearrange("p (h d) -> p h d", h=32)
            tt(DVE, hv, hv, bc32(dd_t[:]), ALU.mult, [h_b, dd_b], [h_b])
            stt(DVE, h_t[:], st_t[:], mcol, h_t[:], ALU.mult, ALU.add, [st_b, mk_b, h_b], [h_b])
        hin.append((h_t, h_b))

    ep_t, ep_b = sb([128, NCH, 32], F32, "epre")
    es_t, es_b = sb([128, NCH, 32], F32, "esuf")
    fw.op(DVE, lambda e: e.memset(ep_t[:, 0, :], 0.0), [], [ep_b])
    fw.op(DVE, lambda e: e.memset(es_t[:, NCH - 1, :], 0.0), [], [es_b])
    for c in range(1, NCH):
        tt(DVE, ep_t[:, c, :], ep_t[:, c - 1, :], tot_t[:, c - 1, 0:32], ALU.add, [ep_b, tot_b], [ep_b])
    for c in range(NCH - 2, -1, -1):
        tt(DVE, es_t[:, c, :], es_t[:, c + 1, :], tot_t[:, c + 1, 32:64], ALU.add, [es_b, tot_b], [es_b])
    act(ep_t[:], ep_t[:], AF.Exp, [ep_b], [ep_b])
    act(es_t[:], es_t[:], AF.Exp, [es_b], [es_b])
    wfc_t, wfc_b = sb([128, NCH, 32], F32, "wfc")
    tt(DVE, wfc_t[:], tot_t[:, :, 0:32], cs2_t[:, :, 0:32], ALU.subtract, [tot_b, cs2_b], [wfc_b])
    act(wfc_t[:], wfc_t[:], AF.Exp, [wfc_b], [wfc_b])
    tt(DVE, wfc_t[:], wfc_t[:], dt_t[:, :, 0:32], ALU.mult, [wfc_b, dt_b], [wfc_b])

    hfl_t, hfl_b = sb([128, 2048], F32, "hfl")
    fw.op(DVE, lambda e: e.memset(hfl_t[:], 0.0), [], [hfl_b])
    hflb_t, hflb_b = sb([128, 2048], BF16, "hflb")
    fw.op(POOL, lambda e: e.memset(hflb_t[:], 0.0), [], [hflb_b])
    hind = [sb([128, 2048], BF16, "hind") for _ in range(2)]
    hblc = [sb([128, 2048], BF16, "hblc") for _ in range(1)]
    sz_t, sz_b = sb([128, 2048], BF16, "sz")
    un_t, un_b = sb([128, 2048], BF16, "un")
    uT_t, uT_b = sb([128, 16, 128], BF16, "uT")
    cbm = [sb([128, 2, 256], F32, "cbm") for _ in range(1)]
    Tm = [sb([128, 512], F32, "Tm") for _ in range(1)]
    Em = [sb([128, 512], F32, "Em") for _ in range(1)]
    Mq = [sb([128, 4, 512], BF16, "Mq") for _ in range(1)]
    t1 = [sb([128, 512], F32, "t1") for _ in range(1)]
    t2 = [sb([128, 512], F32, "t2") for _ in range(1)]
    t3 = [sb([128, 512], F32, "t3") for _ in range(1)]
    qT_t, qT_b = sb([128, 8, 128], BF16, "qT")
    kT_t, kT_b = sb([128, 4, 384], BF16, "kT")
    v_t, v_b = sb([128, 3, 256], BF16, "v")
    lg = [sb([128, 384], F32, "lg") for _ in range(1)]
    pn = [sb([128, 384], BF16, "pn") for _ in range(1)]
    pT = [sb([128, 384], BF16, "pT") for _ in range(1)]
    sm = [sb([128, 8], F32, "sm") for _ in range(2)]
    oT_t, oT_b = sb([64, 16, 128], BF16, "oT")
    mxT_t, mxT_b = sb([128, 8, 128], BF16, "mxT")
    gsb = [sb([128, 128], F32, "gs") for _ in range(2)]
    mxa = [sb([128, 128], F32, "mxa") for _ in range(2)]
    xres = xin
    mo = [sb([128, D], F32, "mo") for _ in range(1)]
    x1_b = Buf("x1d")
    U_dir = [U_incl, U_rev]
    m_dir = [maskF, maskB]

    for c in range(NCH):
        hT_t, hw_b = load_window(c)
        hbc = [hw_b]
        x_t, x_b, bk_t, bk_b, bT_t, bT_b, cT_t, cT_b = xbc_chunk(c, True, hT_t, hw_b)
        nb_t, nb_b, ee_t, ee_b, et_t, etot_b = chunk_small(c)
        for zb in range(4):
            wv, wb = load_w(w_in[:, OFF_Z + zb * 512:OFF_Z + (zb + 1) * 512], 8, 512)
            p_t, p_b = psum()
            for k in range(8):
                mm(p_t[:], hT_t[:, k, col:col + 128], wv[:, k, :], k == 0, k == 7, hbc + [wb], [p_b])
            act(sz_t[:, zb * 512:(zb + 1) * 512], p_t[:], AF.Silu, [p_b], [sz_b])
        hdf_t, hdf_b = hind[0]
        hdb_t, hdb_b = hind[1]
        tt(DVE, hdf_t[:].rearrange("p (h d) -> p h d", h=32), hin[0][0][:].rearrange("p (h d) -> p h d", h=32),
           bc32(ep_t[:, c, :]), ALU.mult, [hin[0][1], ep_b], [hdf_b])
        tt(POOL, hdb_t[:].rearrange("p (h d) -> p h d", h=32), hin[1][0][:].rearrange("p (h d) -> p h d", h=32),
           bc32(es_t[:, c, :]), ALU.mult, [hin[1][1], es_b], [hdb_b])
        hl_t, hl_b = hblc[0]
        fw.dma(SP, hl_t[:], hbl_in[c * 128:(c + 1) * 128, :], writes=[hl_b])
        for q in range(4):
            qs = slice(q * 512, (q + 1) * 512)
            hs = slice(q * 8, (q + 1) * 8)
            p_t, p_b = psum()
            for j in range(2):
                g = 2 * q + j
                mm(p_t[:, j * 128:(j + 1) * 128], bT_t[:, g, :], cT_t[:, g, :], True, True, [bT_b, cT_b], [p_b])
            cb_t2, cb_b2 = cbm[0]
            for d in range(2):
                tt(DVE, cb_t2[:, d, :].rearrange("p (j l) -> p j l", j=2),
                   p_t[:, 0:256].rearrange("p (j l) -> p j l", j=2),
                   m_dir[d].unsqueeze(1).to_broadcast([128, 2, 128]), ALU.mult, [p_b, cs_b], [cb_b2])
            M_t, M_b = Mq[0]
            for d in range(2):
                for j in range(2):
                    g = 2 * q + j
                    hd0 = d * 32 + g * 4
                    p2_t, p2_b = psum()
                    for r in range(4):
                        mm(p2_t[:, r * 128:(r + 1) * 128], adt_t[:, c, hd0 + r:hd0 + r + 1].to_broadcast([128, 128]),
                           U_dir[d], True, True, [adt_b, cs_b], [p2_b])
                    T_t, T_b = Tm[0]
                    tt(DVE, T_t[:].rearrange("p (r l) -> p r l", r=4), p2_t[:].rearrange("p (r l) -> p r l", r=4),
                       cs2_t[:, c, hd0:hd0 + 4].unsqueeze(2).to_broadcast([128, 4, 128]), ALU.min,
                       [p2_b, cs2_b], [T_b])
                    E_t, E_b = Em[0]
                    for r in range(4):
                        act(E_t[:, r * 128:(r + 1) * 128], T_t[:, r * 128:(r + 1) * 128], AF.Exp, [T_b, nb_b], [E_b],
                            bias=nb_t[:, hd0 + r:hd0 + r + 1])
                    tt(POOL, M_t[:, d * 2 + j, :].rearrange("p (r l) -> p r l", r=4),
                       E_t[:].rearrange("p (r l) -> p r l", r=4),
                       cb_t2[:, d, j * 128:(j + 1) * 128].unsqueeze(1).to_broadcast([128, 4, 128]), ALU.mult,
                       [E_b, cb_b2], [M_b])
            py_t, py_b = psum()
            for j in range(2):
                for r in range(4):
                    hl = j * 4 + r
                    h = q * 8 + hl
                    mm(py_t[:, hl * 64:(hl + 1) * 64], M_t[:, j, r * 128:(r + 1) * 128], x_t[:, h * 64:(h + 1) * 64],
                       True, False, [M_b, x_b], [py_b])
                    mm(py_t[:, hl * 64:(hl + 1) * 64], M_t[:, 2 + j, r * 128:(r + 1) * 128],
                       x_t[:, h * 64:(h + 1) * 64], False, True, [M_b, x_b], [py_b])
            pf_t, pf_b = psum()
            pb_t, pb_b = psum()
            for j in range(2):
                g = 2 * q + j
                gs_ = slice(g * 256, (g + 1) * 256)
                mm(pf_t[:, j * 256:(j + 1) * 256], cT_t[:, g, :], hflb_t[:, gs_], True, False, [cT_b, hflb_b], [pf_b])
                mm(pf_t[:, j * 256:(j + 1) * 256], cT_t[:, g, :], hdf_t[:, gs_], False, True, [cT_b, hdf_b], [pf_b])
                mm(pb_t[:, j * 256:(j + 1) * 256], cT_t[:, g, :], hl_t[:, gs_], True, False, [cT_b, hl_b], [pb_b])
                mm(pb_t[:, j * 256:(j + 1) * 256], cT_t[:, g, :], hdb_t[:, gs_], False, True, [cT_b, hdb_b], [pb_b])
            a1, a1b = t1[0]
            a2, a2b = t2[0]
            a3, a3b = t3[0]

            def v8(ap):
                return ap.rearrange("p (h d) -> p h d", h=8)

            def b8(ap2d):
                return ap2d.unsqueeze(2).to_broadcast([128, 8, 64])

            tt(DVE, v8(a1[:]), v8(pf_t[:]), b8(ee_t[:, q * 8:(q + 1) * 8]), ALU.mult, [pf_b, ee_b], [a1b])
            tt(DVE, v8(a2[:]), v8(pb_t[:]), b8(ee_t[:, 32 + q * 8:32 + (q + 1) * 8]), ALU.mult, [pb_b, ee_b], [a2b])
            tt(POOL, v8(a3[:]), v8(x_t[:, qs]), b8(dsk[:, hs]), ALU.mult, [x_b, rp_b], [a3b])
            tt(POOL, a1[:], a1[:], a2[:], ALU.add, [a1b, a2b], [a1b])
            tt(POOL, a1[:], a1[:], a3[:], ALU.add, [a1b, a3b], [a1b])
            tt(DVE, a1[:], a1[:], py_t[:], ALU.add, [a1b, py_b], [a1b])
            tt(DVE, u_t[:, qs], a1[:], sz_t[:, qs], ALU.mult, [a1b, sz_b], [u_b])
            xw_t, xw_b = xw[q % 2]
            tt(POOL, v8(xw_t[:, 0:512]), v8(x_t[:, qs]), b8(wfc_t[:, c, hs]), ALU.mult, [x_b, wfc_b], [xw_b])
            ps_t, ps_b = psum()
            for j in range(2):
                g = 2 * q + j
                mm(ps_t[:, j * 256:(j + 1) * 256], bk_t[:, g * 128:(g + 1) * 128], xw_t[:, j * 256:(j + 1) * 256],
                   True, True, [bk_b, xw_b], [ps_b])
            tt(DVE, v8(hfl_t[:, qs]), v8(hfl_t[:, qs]), b8(et_t[:, hs]), ALU.mult, [hfl_b, etot_b], [hfl_b])
            tt(DVE, hfl_t[:, qs], hfl_t[:, qs], ps_t[:], ALU.add, [hfl_b, ps_b], [hfl_b])
            cp(ACT, hflb_t[:, qs], hfl_t[:, qs], [hfl_b], [hflb_b])
        if DEBUG:
            fw.dma(SP, dbg_u[c * 128:(c + 1) * 128, :], u_t[:], reads=[u_b], writes=[dbg_b], owner=u_b)
        r8_t, r8_b = rms_rstd(u_t[:], u_b, 8, 256)
        tt(DVE, un_t[:].rearrange("p (g d) -> p g d", g=8), u_t[:].rearrange("p (g d) -> p g d", g=8),
           r8_t.unsqueeze(2).to_broadcast([128, 8, 256]), ALU.mult, [u_b, r8_b], [un_b])
        for hf in range(2):
            p_t, p_b = psum()
            pv = p_t[:].bitcast(BF16)
            for k in range(8):
                i = hf * 8 + k
                tr(pv[:, k * 128:(k + 1) * 128], un_t[:, i * 128:(i + 1) * 128], identb, [un_b, idb_b], [p_b])
            tt(DVE, uT_t[:, hf * 8:(hf + 1) * 8, :], pv.rearrange("p (k c) -> p k c", k=8),
               nw_t[:, hf * 8:(hf + 1) * 8].unsqueeze(2).to_broadcast([128, 8, 128]), ALU.mult, [p_b, nw_b], [uT_b])

        for blk in range(2):
            wv, wb = load_w(w_in[:, OFF_Q + blk * 512:OFF_Q + (blk + 1) * 512], 8, 512)
            for j in range(4):
                p_t, p_b = psum()
                for k in range(8):
                    mm(p_t[:, 0:128], wv[:, k, j * 128:(j + 1) * 128], hT_t[:, k, col:col + 128], k == 0, k == 7,
                       hbc + [wb], [p_b])
                act(qT_t[:, blk * 4 + j, :], p_t[:, 0:128], AF.Copy, [p_b], [qT_b], scale=0.125)
        wv, wb = load_w(w_in[:, OFF_K:OFF_K + 512], 8, 512)
        hb3 = [hw_b]
        kd_t, kd_b = wpool[wi[0] % NW]
        wi[0] += 1
        kdv = kd_t[:, 0:8 * 512].rearrange("p (k c) -> p k c", k=8)
        for g in range(4):
            for half in range(2):
                fw.dma(POOL, kdv[:, :, g * 128 + half * 64:g * 128 + half * 64 + 64],
                       w_in[:, OFF_K + g * 64:OFF_K + (g + 1) * 64].rearrange("(k p) c -> p k c", p=128),
                       writes=[kd_b])
        for g in range(4):
            p_t, p_b = psum()
            for k in range(8):
                mm(p_t[:, 0:384], kdv[:, k, g * 128:(g + 1) * 128], hT_t[:, k, col - 128:col + 256], k == 0, k == 7,
                   hb3 + [kd_b], [p_b])
            cp(ACT, kT_t[:, g, :], p_t[:, 0:384], [p_b], [kT_b])
        for jt in range(3):
            p_t, p_b = psum()
            cc = col - 128 + jt * 128
            for k in range(8):
                mm(p_t[:, 0:256], hT_t[:, k, cc:cc + 128], wv[:, k, 256:512], k == 0, k == 7, hb3 + [wb], [p_b])
            cp(ACT, v_t[:, jt, :], p_t[:, 0:256], [p_b], [v_b])
        for hq in range(16):
            g = hq // 4
            po = (hq % 2) * 64
            p_t, p_b = psum()
            mm(p_t[:, 0:384], qT_t[po:po + 64, hq // 2, :], kT_t[po:po + 64, g, :], True, True, [qT_b, kT_b], [p_b])
            l_t, l_b = lg[0]
            tt(DVE, l_t[:], p_t[:, 0:384], bias_t[:, hq, :], ALU.add, [p_b, bias_b], [l_b])
            if c == 0:
                ts(DVE, l_t[:, 0:128], l_t[:, 0:128], mk_t[:, 0:1], None, ALU.add, ALU.bypass, [l_b, mk_b], [l_b])
            if c == NCH - 1:
                ts(DVE, l_t[:, 256:384], l_t[:, 256:384], mk_t[:, 1:2], None, ALU.add, ALU.bypass, [l_b, mk_b], [l_b])
            s_t, s_b = sm[hq % 2]
            fw.op(DVE, lambda e, s_t=s_t, l_t=l_t: e.reduce_max(out=s_t[:, 0:1], in_=l_t[:], axis=AX.X), [l_b], [s_b])
            tt(DVE, s_t[:, 0:1], s_t[:, 0:1], sink[:, hq:hq + 1], ALU.max, [s_b, rp_b], [s_b])
            ts(DVE, s_t[:, 1:2], s_t[:, 0:1], -1.0, None, ALU.mult, ALU.bypass, [s_b], [s_b])
            act(l_t[:], l_t[:], AF.Exp, [l_b, s_b], [l_b, s_b], bias=s_t[:, 1:2], accum_out=s_t[:, 2:3])
            act(s_t[:, 3:4], sink[:, hq:hq + 1], AF.Exp, [rp_b, s_b], [s_b], bias=s_t[:, 1:2])
            tt(DVE, s_t[:, 4:5], s_t[:, 2:3], s_t[:, 3:4], ALU.add, [s_b], [s_b])
            recip(s_t[:, 5:6], s_t[:, 4:5], [s_b], [s_b])
            n_t, n_b = pn[0]
            ts(POOL, n_t[:], l_t[:], s_t[:, 5:6], None, ALU.mult, ALU.bypass, [l_b, s_b], [n_b])
            p2_t, p2_b = psum()
            pv = p2_t[:].bitcast(BF16)
            for jt in range(3):
                tr(pv[:, jt * 128:(jt + 1) * 128], n_t[:, jt * 128:(jt + 1) * 128], identb, [n_b, idb_b], [p2_b])
            pt_t, pt_b = pT[0]
            cp(ACT, pt_t[:], pv[:, 0:384], [p2_b], [pt_b])
            p3_t, p3_b = psum()
            for jt in range(3):
                mm(p3_t[0:64, 0:128], v_t[:, jt, g * 64:(g + 1) * 64], pt_t[:, jt * 128:(jt + 1) * 128], jt == 0,
                   jt == 2, [v_b, pt_b], [p3_b])
            cp(ACT, oT_t[:, hq, :], p3_t[0:64, 0:128], [p3_b], [oT_b])

        if DEBUG:
            fw.dma(SP, dbg_o[c * 64:(c + 1) * 64, :], oT_t[:].rearrange("p h c -> p (h c)"), reads=[oT_b], writes=[dbg_b], owner=oT_b)
        for blk in range(4):
            cs_ = slice(blk * 256, (blk + 1) * 256)
            wso_v, wso_b = load_w(w_so[:, cs_], 16, 256)
            wgs_v, wgs_b = load_w(w_in[:, OFF_G + blk * 256:OFF_G + (blk + 1) * 256], 8, 256)
            for j in range(2):
                i = blk * 2 + j
                js = slice(j * 128, (j + 1) * 128)
                pa_t, pa_b = psum()
                for k in range(16):
                    mm(pa_t[:, 0:128], wso_v[:, k, js], uT_t[:, k, :], k == 0, k == 15, [wso_b, uT_b], [pa_b])
                pg_t, pg_b = psum()
                for k in range(8):
                    mm(pg_t[:, 0:128], wgs_v[:, k, js], hT_t[:, k, col:col + 128], k == 0, k == 7, hbc + [wgs_b], [pg_b])
                g_t, g_b = gsb[0]
                act(g_t[:], pg_t[:, 0:128], AF.Sigmoid, [pg_b, bg_b], [g_b], bias=bg_t[:, i:i + 1])
                m_t, m_b = mxa[j]
                tt(DVE, m_t[:], g_t[:], pa_t[:, 0:128], ALU.mult, [g_b, pa_b], [m_b])
            wao_t, wao_b = wpool[wi[0] % NW]
            wi[0] += 1
            wao_v = wao_t[0:64, 0:16 * 256].rearrange("p (h c) -> p h c", h=16)
            fw.dma(POOL, wao_v, w_ao[:, cs_].rearrange("(h p) c -> p h c", p=64), writes=[wao_b])
            wga_v, wga_b = load_w(w_in[:, OFF_G + 1024 + blk * 256:OFF_G + 1024 + (blk + 1) * 256], 8, 256)
            for j in range(2):
                i = blk * 2 + j
                js = slice(j * 128, (j + 1) * 128)
                m_t, m_b = mxa[j]
                pc_t, pc_b = psum()
                for h in range(16):
                    mm(pc_t[:, 0:128], wao_v[:, h, js], oT_t[:, h, :], h == 0, h == 15, [wao_b, oT_b], [pc_b])
                pd_t, pd_b = psum()
                for k in range(8):
                    mm(pd_t[:, 0:128], wga_v[:, k, js], hT_t[:, k, col:col + 128], k == 0, k == 7, hbc + [wga_b], [pd_b])
                g2_t, g2_b = gsb[1]
                act(g2_t[:], pd_t[:, 0:128], AF.Sigmoid, [pd_b, bg_b], [g2_b], bias=bg_t[:, 8 + i:9 + i])
                tt(DVE, g2_t[:], g2_t[:], pc_t[:, 0:128], ALU.mult, [g2_b, pc_b], [g2_b])
                tt(DVE, mxT_t[:, i, :], m_t[:], g2_t[:], ALU.add, [m_b, g2_b], [mxT_b])
        if DEBUG:
            fw.dma(SP, dbg_m[c * 128:(c + 1) * 128, :], mxT_t[:].rearrange("p h c -> p (h c)"), reads=[mxT_b], writes=[dbg_b], owner=mxT_b)
        mo_t, mo_b = mo[0]
        for half in range(2):
            wv, wb = load_w(w_o[:, half * 512:(half + 1) * 512], 8, 512)
            p_t, p_b = psum()
            for k in range(8):
                mm(p_t[:], mxT_t[:, k, :], wv[:, k, :], k == 0, k == 7, [mxT_b, wb], [p_b])
            cp(ACT, mo_t[:, half * 512:(half + 1) * 512], p_t[:], [p_b], [mo_b])
        r_t, r_b = rms_rstd(mo_t[:], mo_b, 1, D)
        xr_t, xr_b = xres[c % 2]
        fw.dma(SP, xr_t[:], xh[128 + c * 128:256 + c * 128, :], writes=[xr_b])
        stt(DVE, mo_t[:], mo_t[:], r_t[:, 0:1], pmix, ALU.mult, ALU.mult, [mo_b, r_b, rp_b], [mo_b])
        tt(DVE, xr_t[:], xr_t[:], mo_t[:], ALU.add, [xr_b, mo_b], [xr_b])
        fw.dma(SP, x1d[c * 128:(c + 1) * 128, :], xr_t[:], reads=[xr_b], writes=[x1_b], owner=xr_b)

    h2T = [(un_t[:, 0:1024].rearrange("p (k c) -> p k c", k=8), un_b)]
    aT = [(u_t[:].bitcast(BF16).rearrange("p (k c) -> p k c", k=32), u_b)]
    y_b = Buf("yout")
    for c in range(NCH):
        xr_t, xr_b = xres[c % 2]
        fw.dma(SP, xr_t[:], x1d[c * 128:(c + 1) * 128, :], reads=[x1_b], writes=[xr_b])
        h_t, h_b = h2T[0]
        norm_transpose(xr_t[:], xr_b, g3_t[:], g3_b, h_t, h_b)
        a_t3, a_b3 = aT[0]
        for blk in range(8):
            wv, wb = load_w(w1[:, blk * 512:(blk + 1) * 512], 8, 512)
            for j in range(4):
                p_t, p_b = psum()
                for k in range(8):
                    mm(p_t[:, 0:128], wv[:, k, j * 128:(j + 1) * 128], h_t[:, k, :], k == 0, k == 7, [wb, h_b], [p_b])
                r_t2, r_b2 = gsb[j % 2]
                act(r_t2[:], p_t[:, 0:128], AF.Relu, [p_b], [r_b2])
                tt(DVE if j % 2 == 0 else POOL, a_t3[:, blk * 4 + j, :], r_t2[:], r_t2[:], ALU.mult, [r_b2], [a_b3])
        pA_t, pA_b = psum()
        pB_t, pB_b = psum()
        for blk in range(8):
            wv, wb = load_w(w2[blk * 512:(blk + 1) * 512, :], 4, 1024)
            for f in range(4):
                ff = blk * 4 + f
                mm(pA_t[:], a_t3[:, ff, :], wv[:, f, 0:512], ff == 0, ff == 31, [a_b3, wb], [pA_b], inc=(ff == 31))
                mm(pB_t[:], a_t3[:, ff, :], wv[:, f, 512:1024], ff == 0, ff == 31, [a_b3, wb], [pB_b],
                   inc=(f == 3 or ff == 31))
        mo_t, mo_b = mo[0]
        cp(ACT, mo_t[:, 0:512], pA_t[:], [pA_b], [mo_b])
        cp(ACT, mo_t[:, 512:1024], pB_t[:], [pB_b], [mo_b])
        r_t, r_b = rms_rstd(mo_t[:], mo_b, 1, D)
        stt(DVE, mo_t[:], mo_t[:], r_t[:, 0:1], pmlp, ALU.mult, ALU.mult, [mo_b, r_b, rp_b], [mo_b])
        tt(DVE, xr_t[:], xr_t[:], mo_t[:], ALU.add, [xr_b, mo_b], [xr_b])
        fw.dma(SP, yout[c * 128:(c + 1) * 128, :], xr_t[:], reads=[xr_b], writes=[y_b], owner=xr_b)
    endt, endb = sb([128, 1], F32, "end")
    fw.op(POOL, lambda e: e.memset(endt[:], 0.0), [], [endb])
    fw.op(ACT, lambda e: e.copy(out=endt[:], in_=endt[:]), [endb], [endb])
    fw.op(DVE, lambda e: e.tensor_copy(out=endt[:], in_=endt[:]), [endb], [endb])
    fw.finish([y_b, endb, dbg_b, u_b, oT_b, mxT_b] + [b for _, b in xres])
    nc._fw_counts = (list(fw.ecnt), [len(x) for x in fw.q], fw.nsem)
    return nc


def _t5_bucket(rel):
    nb = 16
    max_exact = 8
    ret = np.where(rel > 0, nb, 0)
    n = np.abs(rel)
    nf = np.maximum(n, 1).astype(np.float32)
    large = max_exact + (np.log(nf / max_exact) / math.log(128 / max_exact) * (nb - max_exact)).astype(np.int32)
    large = np.minimum(large, nb - 1)
    return ret + np.where(n < max_exact, n, large)


def _consts():
    I = np.eye(128, dtype=np.float32)
    tri = np.triu(np.ones((128, 128), np.float32))
    cst = np.concatenate([I, I[::-1], tri, tri.T, tri, tri.T, np.ones((128, 128), np.float32)], axis=1)
    rel = np.arange(511) - 255
    bk = _t5_bucket(rel)
    oh = np.zeros((33, 512), np.float32)
    for idx in range(511):
        if abs(rel[idx]) <= 128:
            oh[bk[idx], idx] = 1.0
        else:
            oh[32, idx] = 1.0
    oh[32, 511] = 1.0
    return np.ascontiguousarray(cst), oh


_PROGS = {}


def _prog(phase):
    if phase not in _PROGS:
        _PROGS[phase] = build(phase)
    return _PROGS[phase]


def kernel(x, pre_mix_norm, w_in, b_gate, conv_w, conv_b, dt_bias, a_log, d_skip, ssd_norm,
           w_ssd_out, attn_sink, rel_bias_table, w_attn_out, w_o, post_mix_norm,
           pre_mlp_norm, w_mlp_in, w_mlp_out, post_mlp_norm):
    import ml_dtypes
    f = lambda a: np.ascontiguousarray(np.asarray(a), dtype=np.float32)
    x = f(x)
    cst, oh = _consts()
    tb = np.concatenate([f(rel_bias_table), np.full((1, 16), NEG, np.float32)], axis=0)
    p128 = lambda v, k: np.ascontiguousarray(f(v).reshape(k, 128).T)
    cur = x
    for l in range(2):
        xp = np.pad(cur, ((0, 0), (128, 128), (0, 0)))
        rowp = np.concatenate([f(dt_bias[l]).reshape(-1), f(a_log[l]).reshape(-1), f(d_skip[l]).reshape(-1),
                               f(attn_sink[l]).reshape(-1), f(post_mix_norm[l]), f(post_mlp_norm[l])])[None, :]
        cw = np.ascontiguousarray(f(conv_w[l])[:, 0, :].reshape(5, 32, 128).transpose(2, 1, 0).reshape(128, 160))
        common = {"w_in": f(w_in[l]), "cw": cw, "cb": p128(conv_b[l], 32), "g1": p128(pre_mix_norm[l], 8),
                  "rowp": np.ascontiguousarray(rowp), "cst": cst}
        ins1 = []
        for c in range(8):
            b, pos = c // 4, c % 4
            xh = np.ascontiguousarray(xp[b, pos * 2048:pos * 2048 + 2304])
            ins1.append(dict(common, xh=xh))
        r1 = run_bass_kernel_spmd(_prog(1), ins1, core_ids=list(range(8))).results
        ins2 = []
        for c in range(8):
            b, pos = c // 4, c % 4
            mk = np.zeros((128, 16), np.float32)
            mk[:, 0] = NEG if pos == 0 else 0.0
            mk[:, 1] = NEG if pos == 3 else 0.0
            for r in range(4):
                mk[:, 2 + r] = 1.0 if r < pos else 0.0
                mk[:, 6 + r] = 1.0 if r > pos else 0.0
            sf4 = np.concatenate([r1[b * 4 + r]["hf"] for r in range(4)], axis=0)
            sb4 = np.concatenate([r1[b * 4 + r]["hb"] for r in range(4)], axis=0)
            dec4 = np.concatenate([r1[b * 4 + r]["dec"] for r in range(4)], axis=1)
            d2 = dict(ins1[c])
            d2.update({"w_so": f(w_ssd_out[l]), "w_ao": f(w_attn_out[l]), "w_o": f(w_o[l]), "w1": f(w_mlp_in[l]),
                       "w2": f(w_mlp_out[l]), "g3": p128(pre_mlp_norm[l], 8), "nw": p128(ssd_norm[l], 16),
                       "bg": p128(b_gate[l], 16), "oh": oh, "tb": tb, "mk": mk,
                       "sf4": np.ascontiguousarray(sf4), "sb4": np.ascontiguousarray(sb4),
                       "dec4": np.ascontiguousarray(dec4), "hbl": np.ascontiguousarray(r1[c]["hbl"])})
            ins2.append(d2)
        r2 = run_bass_kernel_spmd(_prog(2), ins2, core_ids=list(range(8))).results
        nxt = np.zeros_like(cur)
        for c in range(8):
            b, pos = c // 4, c % 4
            nxt[b, pos * 2048:(pos + 1) * 2048] = r2[c]["y"]
        cur = nxt
    return cur.astype(np.float32)
```

```python
import math
from contextlib import ExitStack
import numpy as np
import concourse.bass as bass
import concourse.mybir as mybir
from concourse.bass_utils import run_bass_kernel_spmd

F32 = mybir.dt.float32
BF16 = mybir.dt.bfloat16
AF = mybir.ActivationFunctionType
ALU = mybir.AluOpType
AX = mybir.AxisListType
PE, ACT, DVE, POOL, SP = range(5)
SAME_ENGINE_SYNC = True
DEBUG = False

D = 1024
NIN = 9792
OFF_Z, OFF_X, OFF_B, OFF_C, OFF_DT, OFF_Q, OFF_K, OFF_V, OFF_G = 0, 2048, 4096, 5120, 6144, 6208, 7232, 7488, 7744
NEG = -30000.0
EPS = 1e-6
NCH = 16


class Buf:
    __slots__ = ("name", "wr", "rd", "dsem", "dcnt")

    def __init__(self, name=""):
        self.name = name
        self.wr = None
        self.rd = {}
        self.dsem = None
        self.dcnt = 0


class FW:
    def __init__(self, nc):
        self.nc = nc
        self.q = [[] for _ in range(5)]
        self.esem = [nc.alloc_semaphore(name=f"es{i}") if i < 4 else None for i in range(5)]
        self.ecnt = [0] * 5
        self.waited = [dict() for _ in range(5)]
        self.nsem = 5
        self.owners = []
        self.ccsem = None
        self.ccn = 0

    def _collect(self, e, reads, writes):
        waits = {}
        wt = self.waited[e]

        def need(ev, same_ok):
            if ev is None:
                return
            sem, val, src = ev
            if src == e and same_ok:
                return
            k = id(sem)
            if wt.get(k, 0) >= val:
                return
            if k not in waits or waits[k][1] < val:
                waits[k] = (sem, val)

        raw_same_ok = (e == PE) or (not SAME_ENGINE_SYNC)
        for b in reads:
            need(b.wr, raw_same_ok)
        for b in writes:
            need(b.wr, raw_same_ok)
            for ev in b.rd.values():
                need(ev, True)
        for k, (sem, val) in waits.items():
            wt[k] = val
        return list(waits.values())

    def op(self, e, fn, reads=(), writes=(), inc=True):
        wl = self._collect(e, reads, writes)
        val = self.ecnt[e] + 1
        if inc:
            self.ecnt[e] += 1
        sem_e = self.esem[e]
        ev = (sem_e, val, e)
        for b in reads:
            b.rd[e] = ev
        for b in writes:
            b.wr = ev
            b.rd = {}

        def emit(eng):
            for sem, v in wl:
                eng.wait_ge(sem, v)
            ins = fn(eng)
            if inc:
                ins.then_inc(sem_e, 1)

        self.q[e].append(emit)

    def dma(self, e, out, in_, reads=(), writes=(), owner=None):
        wl = self._collect(e, reads, writes)
        if owner is None:
            owner = writes[0] if writes else reads[0]
        if owner.dsem is None:
            owner.dsem = self.nc.alloc_semaphore(name=f"ds{self.nsem}")
            self.nsem += 1
        if owner.dcnt == 0:
            self.owners.append(owner)
        owner.dcnt += 16
        sem = owner.dsem
        ev = (sem, owner.dcnt, -1)
        for b in reads:
            b.rd[("d", id(sem))] = ev
        for b in writes:
            b.wr = ev
            b.rd = {}

        def emit(eng):
            for s_, v in wl:
                eng.wait_ge(s_, v)
            eng.dma_start(out=out, in_=in_).then_inc(sem, 16)

        self.q[e].append(emit)

    def close_group(self, grp, bufs):
        for b in bufs:
            b.wr = (grp.dsem, grp.dcnt, -1)

    def collective(self, kind, alu, in_ap, out_ap, dummy_ap, reads, writes):
        e = POOL
        wl = self._collect(e, reads, writes)
        if self.ccsem is None:
            self.ccsem = self.nc.alloc_semaphore(name="ccsem")
        self.ccn += 1
        n = self.ccn
        ccsem = self.ccsem
        val = self.ecnt[e] + 1
        self.ecnt[e] += 1
        sem_e = self.esem[e]
        ev = (sem_e, val, e)
        for b in reads:
            b.rd[e] = ev
        for b in writes:
            b.wr = ev
            b.rd = {}

        def emit(eng):
            for s_, v in wl:
                eng.wait_ge(s_, v)
            eng.collective_compute(kind, alu, replica_groups=[list(range(8))], ins=[in_ap], outs=[out_ap]).then_inc(ccsem)
            eng.wait_ge(ccsem, n)
            eng.memset(dummy_ap, 0.0).then_inc(sem_e, 1)

        self.q[e].append(emit)

    def barrier(self):
        targets = [(self.esem[e2], self.ecnt[e2], e2) for e2 in range(5) if self.ecnt[e2] > 0]
        targets += [(b.dsem, b.dcnt, -1) for b in self.owners]
        for e in range(5):
            wl = []
            for sem, v, src in targets:
                if src == e:
                    continue
                k = id(sem)
                if self.waited[e].get(k, 0) < v:
                    self.waited[e][k] = v
                    wl.append((sem, v))

            def emit(eng, wl=wl):
                for s_, v in wl:
                    eng.wait_ge(s_, v)

            self.q[e].append(emit)
        if max(self.ecnt) > 1000:
            self.gen = getattr(self, "gen", 0) + 1
            self.esem = [self.nc.alloc_semaphore(name=f"es{i}g{self.gen}") if i < 4 else None for i in range(5)]
            self.ecnt = [0] * 5
            keep = []
            for b in self.owners:
                if b.dcnt > 1000:
                    b.dsem = None
                    b.dcnt = 0
                else:
                    keep.append(b)
            self.owners = keep

    def finish(self, final_bufs):
        nc = self.nc
        fin = []
        for b in final_bufs:
            for ev in [b.wr] + list(b.rd.values()):
                if ev is not None:
                    fin.append((ev[0], ev[1]))
        q = self.q
        with nc.Block() as block:
            @block.tensor
            def _(eng):
                for f in q[PE]:
                    f(eng)

            @block.scalar
            def _(eng):
                for f in q[ACT]:
                    f(eng)

            @block.vector
            def _(eng):
                for f in q[DVE]:
                    f(eng)

            @block.gpsimd
            def _(eng):
                for f in q[POOL]:
                    f(eng)

            @block.sync
            def _(eng):
                for f in q[SP]:
                    f(eng)
                for sem, v in fin:
                    eng.wait_ge(sem, v)


def build_fused():
    nc = bass.Bass("TRN2", target_bir_lowering=False)
    fw = FW(nc)

    def din(name, shape, dt=F32):
        return nc.dram_tensor(name, shape, dt, kind="ExternalInput").ap()

    def dout(name, shape, dt=F32):
        return nc.dram_tensor(name, shape, dt, kind="ExternalOutput").ap()

    xh0 = din("xh", [18 * 128, D])
    w_in_a = din("w_in", [2 * D, NIN])
    cw_a = din("cw", [2 * 128, 32 * 5])
    cb_a = din("cb", [2 * 128, 32])
    g1_a = din("g1", [2 * 128, 8])
    rowp_a = din("rowp", [2, 2224])
    cst = din("cst", [128, 7 * 128])
    w_so_a = din("w_so", [2 * 2048, D])
    w_ao_a = din("w_ao", [2 * D, D])
    w_o_a = din("w_o", [2 * D, D])
    w1_a = din("w1", [2 * D, 4096])
    w2_a = din("w2", [2 * 4096, D])
    g3_a = din("g3", [2 * 128, 8])
    nw_a = din("nw", [2 * 128, 16])
    bg_a = din("bg", [2 * 128, 16])
    ohd = din("oh", [33, 512])
    tbd = din("tb", [33, 16])
    mkd = din("mk", [128, 34])
    yout = dout("y", [2048, D])
    xh1 = nc.dram_tensor("xh1", [18 * 128, D], F32).ap()
    x1d = nc.dram_tensor("x1d", [2048, D], F32).ap()
    vreld = nc.dram_tensor("vreld", [16, 512], F32)
    hTd = nc.dram_tensor("hTd", [128, 8, 2304], BF16).ap()
    hbl_d = nc.dram_tensor("hbl_d", [NCH * 128, 2048], BF16).ap()
    sbnc = nc.dram_tensor("sbnc", [128, 4160], F32).ap()
    sgath = nc.dram_tensor("sgath", [8 * 128, 4160], F32).ap()
    xbn = nc.dram_tensor("xbn", [256, D], F32).ap()
    xgt = nc.dram_tensor("xgt", [8 * 256, D], F32).ap()
    sbnc_b, sgath_b, hbld_b, xh1_b, xbn_b, xgt_b = Buf("sbnc"), Buf("sgath"), Buf("hbld"), Buf("xh1"), Buf("xbn"), Buf("xgt")
    DEBUG = False
    cnt = [0]

    scope = [None]

    def sb(shape, dt, name=None):
        cnt[0] += 1
        nm = f"s_{name or 't'}{cnt[0]}"
        if scope[0] is not None:
            return scope[0].enter_context(nc.sbuf_tensor(nm, shape, dt)), Buf(nm)
        return nc.alloc_sbuf_tensor(nm, shape, dt), Buf(nm)

    def mm(out, lhsT, rhs, start, stop, reads, writes, inc=None):
        fw.op(PE, lambda e: e.matmul(out, lhsT=lhsT, rhs=rhs, start=start, stop=stop), reads, writes,
              inc=stop if inc is None else inc)

    def tr(out, in_, ident, reads, writes):
        fw.op(PE, lambda e: e.transpose(out=out, in_=in_, identity=ident), reads, writes)

    def act(out, in_, func, reads, writes, **kw):
        fw.op(ACT, lambda e: e.activation(out=out, in_=in_, func=func, **kw), reads, writes)

    def tt(eng, out, in0, in1, op, reads, writes):
        fw.op(eng, lambda e: e.tensor_tensor(out=out, in0=in0, in1=in1, op=op), reads, writes)

    def ts(eng, out, in0, s1, s2, op0, op1, reads, writes):
        if s2 is None:
            fw.op(eng, lambda e: e.tensor_scalar(out=out, in0=in0, scalar1=s1, scalar2=None, op0=op0), reads, writes)
        else:
            fw.op(eng, lambda e: e.tensor_scalar(out=out, in0=in0, scalar1=s1, scalar2=s2, op0=op0, op1=op1), reads, writes)

    def stt(eng, out, in0, scalar, in1, op0, op1, reads, writes):
        fw.op(eng, lambda e: e.scalar_tensor_tensor(out=out, in0=in0, scalar=scalar, in1=in1, op0=op0, op1=op1),
              reads, writes)

    def cp(eng, out, in_, reads, writes):
        if eng == ACT:
            fw.op(ACT, lambda e: e.copy(out=out, in_=in_), reads, writes)
        else:
            fw.op(eng, lambda e: e.tensor_copy(out=out, in_=in_), reads, writes)

    def recip(out, in_, reads, writes):
        fw.op(DVE, lambda e: e.reciprocal(out=out, in_=in_), reads, writes)

    pst = [nc.alloc_psum_tensor(f"ps{i}", [128, 512], F32) for i in range(8)]
    psb = [Buf(f"ps{i}") for i in range(8)]
    psi = [0]

    def psum():
        i = psi[0] % 8
        psi[0] += 1
        return pst[i], psb[i]

    NW = 2
    wpool = [sb([128, 4096], BF16, "w") for _ in range(NW)]
    wi = [0]

    def load_w(src, k, c):
        t, b = wpool[wi[0] % NW]
        wi[0] += 1
        view = t[:, 0:k * c].rearrange("p (k c) -> p k c", k=k)
        fw.dma(POOL, view, src.rearrange("(k p) c -> p k c", p=128), writes=[b])
        return view, b

    cs_t, cs_b = sb([128, 7 * 128], F32, "cst")
    fw.dma(SP, cs_t[:], cst, writes=[cs_b])
    ident32 = cs_t[:, 0:128]
    antiid = cs_t[:, 128:256]
    U_incl = cs_t[:, 256:384]
    U_rev = cs_t[:, 384:512]
    maskF = cs_t[:, 512:640]
    maskB = cs_t[:, 640:768]
    ones32 = cs_t[:, 768:896]
    idb_t, idb_b = sb([128, 128], BF16, "idb")
    fw.dma(POOL, idb_t[:], cst[:, 0:128], writes=[idb_b])
    identb = idb_t[:]
    junk_t, junk_b = sb([128, 1024], BF16, "junk")

    sspool = [sb([128, 8], F32, "ss") for _ in range(4)]
    ssi = [0]

    def rms_rstd(src, src_b, n, width, eps=EPS):
        ss_full, ss_b = sspool[ssi[0] % 4]
        ssi[0] += 1
        ss_t = ss_full[:, 0:n]
        for g in range(n):
            act(junk_t[:, 0:width], src[:, g * width:(g + 1) * width], AF.Square, [src_b], [junk_b, ss_b],
                accum_out=ss_t[:, g:g + 1])
        act(ss_t, ss_t, AF.Sqrt, [ss_b], [ss_b], bias=eps, scale=1.0 / width)
        recip(ss_t, ss_t, [ss_b], [ss_b])
        return ss_t, ss_b

    yout_b = Buf("yout")
    cdm_t, cdm_b = sb([128, 1], F32, "cdm")

    def emit(phase, l):
        xh = xh0 if l == 0 else xh1
        xh_rd = [] if l == 0 else [xh1_b]
        w_in = w_in_a[l * D:(l + 1) * D, :]
        cwd = cw_a[l * 128:(l + 1) * 128, :]
        cbd = cb_a[l * 128:(l + 1) * 128, :]
        g1d = g1_a[l * 128:(l + 1) * 128, :]
        rowp = rowp_a[l:l + 1, :]
        w_so = w_so_a[l * 2048:(l + 1) * 2048, :]
        w_ao = w_ao_a[l * D:(l + 1) * D, :]
        w_o = w_o_a[l * D:(l + 1) * D, :]
        w1 = w1_a[l * D:(l + 1) * D, :]
        w2 = w2_a[l * 4096:(l + 1) * 4096, :]
        g3d = g3_a[l * 128:(l + 1) * 128, :]
        nwd = nw_a[l * 128:(l + 1) * 128, :]
        bgd = bg_a[l * 128:(l + 1) * 128, :]
        hTd_b = Buf("hTd")
        dbg_b = Buf("dbg")
        rp_t, rp_b = sb([128, 2224], F32, "rowp")
        sg = Buf("setupgrp")
        fw.dma(SP, rp_t[:], rowp.partition_broadcast(128).rearrange("p o c -> p (o c)"), writes=[rp_b], owner=sg)
        dtb = rp_t[:, 0:64]
        alog = rp_t[:, 64:128]
        dsk = rp_t[:, 128:160]
        sink = rp_t[:, 160:176]
        pmix = rp_t[:, 176:1200]
        pmlp = rp_t[:, 1200:2224]
        cw_t, cw_b = sb([128, 160], F32, "cw")
        fw.dma(SP, cw_t[:], cwd, writes=[cw_b], owner=sg)
        cb_t, cb_b = sb([128, 32], F32, "cb")
        fw.dma(SP, cb_t[:], cbd, writes=[cb_b], owner=sg)
        g1_t, g1_b = sb([128, 8], F32, "g1")
        fw.dma(SP, g1_t[:], g1d, writes=[g1_b], owner=sg)
        fw.close_group(sg, [rp_b, cw_b, cb_b, g1_b])
        a_t, a_b = sb([128, 64], F32, "a")
        act(a_t[:], alog, AF.Exp, [rp_b], [a_b])
        ts(DVE, a_t[:], a_t[:], -1.0, None, ALU.mult, ALU.bypass, [a_b], [a_b])


        htile = [sb([128, 8, 128], BF16, "htile") for _ in range(2)]
        xin = [sb([128, D], F32, "xin") for _ in range(2)]
        hb2 = [sb([128, D], BF16, "hb") for _ in range(1)]

        def norm_transpose(src, src_b, gain, gain_b, dst, dst_b):
            r_t, r_b = rms_rstd(src, src_b, 1, D)
            h_t, h_b = hb2[0]
            act(h_t[:], src, AF.Copy, [src_b, r_b], [h_b], scale=r_t[:, 0:1])
            p_t, p_b = psum()
            pv = p_t[:].bitcast(BF16)
            for k in range(8):
                tr(pv[:, k * 128:(k + 1) * 128], h_t[:, k * 128:(k + 1) * 128], identb, [h_b, idb_b], [p_b])
            tt(DVE, dst, pv.rearrange("p (k c) -> p k c", k=8), gain.unsqueeze(2).to_broadcast([128, 8, 128]),
               ALU.mult, [p_b, gain_b], [dst_b])

        def small(n=64):
            return sb([128, NCH, n], F32, "dtq")

        dt_t, dt_b = small()
        adt_t, adt_b = small()
        cs2_t, cs2_b = small()
        tot_t, tot_b = small()
        nbq = [sb([128, 64], F32, "nb") for _ in range(2)]
        eeq = [sb([128, 64], F32, "ee") for _ in range(2)]
        etq = [sb([128, 64], F32, "et") for _ in range(2)]

        def chunk_small(c):
            nb_t, nb_b = nbq[c % 2]
            ee_t, ee_b = eeq[c % 2]
            et_t, et_b = etq[c % 2]
            act(nb_t[:], dt_t[:, c, :], AF.Ln, [dt_b], [nb_b])
            tt(DVE, nb_t[:], nb_t[:], cs2_t[:, c, :], ALU.subtract, [nb_b, cs2_b], [nb_b])
            act(ee_t[:], cs2_t[:, c, :], AF.Exp, [cs2_b], [ee_b])
            act(et_t[:], tot_t[:, c, :], AF.Exp, [tot_b], [et_b])
            return nb_t, nb_b, ee_t, ee_b, et_t, et_b

        wdt_v, wdt_b = load_w(w_in[:, OFF_DT:OFF_DT + 64], 8, 64)
        for t in range(18):
            x_t, x_b = xin[t % 2]
            fw.dma(SP, x_t[:], xh[t * 128:(t + 1) * 128, :], reads=xh_rd, writes=[x_b], owner=x_b)
            ht_t, ht_b = htile[t % 2]
            norm_transpose(x_t[:], x_b, g1_t[:], g1_b, ht_t[:], ht_b)
            fw.dma(SP, hTd[:, :, t * 128:(t + 1) * 128], ht_t[:], reads=[ht_b], writes=[hTd_b], owner=ht_b)
            if 1 <= t <= 16:
                c = t - 1
                p_t, p_b = psum()
                for k in range(8):
                    mm(p_t[:, 0:64], ht_t[:, k, :], wdt_v[:, k, :], k == 0, k == 7, [ht_b, wdt_b], [p_b])
                tt(DVE, dt_t[:, c, :], p_t[:, 0:64], dtb, ALU.add, [p_b, rp_b], [dt_b])
                act(dt_t[:, c, :], dt_t[:, c, :], AF.Exp, [dt_b], [dt_b])
                act(dt_t[:, c, :], dt_t[:, c, :], AF.Ln, [dt_b], [dt_b], bias=1.0)
                tt(DVE, adt_t[:, c, :], dt_t[:, c, :], a_t[:], ALU.mult, [dt_b, a_b], [adt_b])
                p2_t, p2_b = psum()
                mm(p2_t[:, 0:32], U_incl, adt_t[:, c, 0:32], True, True, [cs_b, adt_b], [p2_b])
                mm(p2_t[:, 32:64], U_rev, adt_t[:, c, 32:64], True, True, [cs_b, adt_b], [p2_b])
                mm(p2_t[:, 64:128], ones32, adt_t[:, c, :], True, True, [cs_b, adt_b], [p2_b])
                cp(DVE, cs2_t[:, c, :], p2_t[:, 0:64], [p2_b], [cs2_b])
                cp(DVE, tot_t[:, c, :], p2_t[:, 64:128], [p2_b], [tot_b])

        hwin = [sb([128, 8, 384], BF16, "hwin") for _ in range(2)]
        col = 128

        def load_window(c):
            hw_t, hw_b = hwin[c % 2]
            fw.dma(SP, hw_t[:], hTd[:, :, c * 128:c * 128 + 384], reads=[hTd_b], writes=[hw_b])
            return hw_t, hw_b

        xtok = [sb([128, 2048], BF16, "xtok") for _ in range(1)]
        btok = [sb([128, 1024], BF16, "btok") for _ in range(1)]
        bT = [sb([128, 8, 128], BF16, "bT") for _ in range(1)]
        cT = [sb([128, 8, 128], BF16, "cT") for _ in range(1)]
        usb = [sb([128, 132], F32, "usb") for _ in range(2)]
        cacc = [sb([128, 128], F32, "cacc") for _ in range(2)]
        fT = [sb([128, 4, 128], BF16, "fT") for _ in range(2)]

        def xbc_chunk(c, need_c, hT_t, hw_b):
            x_t, x_b = xtok[0]
            bk_t, bk_b = btok[0]
            bT_t, bT_b = bT[0]
            cT_t, cT_b = cT[0]
            hbs = [hw_b]
            nblk = 8 if need_c else 6
            for blk in range(nblk):
                wv, wb = load_w(w_in[:, OFF_X + blk * 512:OFF_X + (blk + 1) * 512], 8, 512)
                f_t, f_b = fT[blk % 2]
                for j in range(4):
                    i = blk * 4 + j
                    p_t, p_b = psum()
                    for k in range(8):
                        mm(p_t[:, 0:132], wv[:, k, j * 128:(j + 1) * 128], hT_t[:, k, col - 2:col + 130],
                           k == 0, k == 7, hbs + [wb], [p_b])
                    u_t, u_b = usb[i % 2]
                    cp(ACT, u_t[:], p_t[:, 0:132], [p_b], [u_b])
                    a_t2, a_b2 = cacc[i % 2]
                    eng = DVE
                    ts(eng, a_t2[:], u_t[:, 0:128], cw_t[:, i * 5:i * 5 + 1], cb_t[:, i:i + 1], ALU.mult, ALU.add,
                       [u_b, cw_b, cb_b], [a_b2])
                    for kk in range(1, 5):
                        stt(eng, a_t2[:], u_t[:, kk:kk + 128], cw_t[:, i * 5 + kk:i * 5 + kk + 1], a_t2[:],
                            ALU.mult, ALU.add, [u_b, cw_b, a_b2], [a_b2])
                    if i < 16:
                        act(f_t[:, j, :], a_t2[:], AF.Silu, [a_b2], [f_b])
                    elif i < 24:
                        act(bT_t[:, i - 16, :], a_t2[:], AF.Silu, [a_b2], [bT_b])
                    else:
                        act(cT_t[:, i - 24, :], a_t2[:], AF.Silu, [a_b2], [cT_b])
                if blk < 6:
                    p_t, p_b = psum()
                    pv = p_t[:].bitcast(BF16)
                    for j in range(4):
                        src = f_t[:, j, :] if blk < 4 else bT_t[:, (blk - 4) * 4 + j, :]
                        srcb = f_b if blk < 4 else bT_b
                        tr(pv[:, j * 128:(j + 1) * 128], src, identb, [srcb, idb_b], [p_b])
                    if blk < 4:
                        cp(ACT, x_t[:, blk * 512:(blk + 1) * 512], pv[:, 0:512], [p_b], [x_b])
                    else:
                        cp(ACT, bk_t[:, (blk - 4) * 512:(blk - 3) * 512], pv[:, 0:512], [p_b], [bk_b])
            return x_t, x_b, bk_t, bk_b, bT_t, bT_b, cT_t, cT_b

        def bc32(ap2d):
            return ap2d.unsqueeze(2).to_broadcast([128, 32, 64])

        xw = [sb([128, 2048 if phase == 1 else 512], BF16, "xw") for _ in range(2)]
        finals = []

        if phase == 1:
            hf_t, hf_b = sb([128, 2048], F32, "hf")
            hb_t, hb_b = sb([128, 2048], F32, "hbs")
            hbb = [sb([128, 2048], BF16, "hbb") for _ in range(2)]
            suf_t, suf_b = sb([128, 32], F32, "suf")
            wf_t, wf_b = sb([128, 64], F32, "wf")
            fw.op(DVE, lambda e: e.memset(suf_t[:], 0.0), [], [suf_b])
            fw.op(DVE, lambda e: e.memset(hb_t[:], 0.0), [], [hb_b])
            fw.op(DVE, lambda e: e.memset(hf_t[:], 0.0), [], [hf_b])
            hbl_b = hbld_b
            for c in range(NCH - 1, -1, -1):
                x_t, x_b, bk_t, bk_b, _, _, _, _ = xbc_chunk(c, False, *load_window(c))
                nb_t, nb_b, ee_t, ee_b, et_t, etot_b = chunk_small(c)
                tt(DVE, wf_t[:, 0:32], tot_t[:, c, 0:32], suf_t[:], ALU.add, [tot_b, suf_b], [wf_b])
                cp(DVE, wf_t[:, 32:64], tot_t[:, c, 32:64], [tot_b], [wf_b])
                tt(DVE, wf_t[:], wf_t[:], cs2_t[:, c, :], ALU.subtract, [wf_b, cs2_b], [wf_b])
                act(wf_t[:], wf_t[:], AF.Exp, [wf_b], [wf_b])
                tt(DVE, wf_t[:], wf_t[:], dt_t[:, c, :], ALU.mult, [wf_b, dt_b], [wf_b])
                tt(DVE, suf_t[:], suf_t[:], tot_t[:, c, 0:32], ALU.add, [suf_b, tot_b], [suf_b])
                hq_t, hq_b = hbb[c % 2]
                cp(ACT, hq_t[:], hb_t[:], [hb_b], [hq_b])
                fw.dma(SP, hbl_d[c * 128:(c + 1) * 128, :], hq_t[:], reads=[hq_b], writes=[hbl_b], owner=hq_b)
                tt(DVE, hb_t[:].rearrange("p (h d) -> p h d", h=32), hb_t[:].rearrange("p (h d) -> p h d", h=32),
                   bc32(et_t[:, 32:64]), ALU.mult, [hb_b, etot_b], [hb_b])
                for d in range(2):
                    xw_t, xw_b = xw[d]
                    eng = DVE if d == 0 else POOL
                    tt(eng, xw_t[:].rearrange("p (h d) -> p h d", h=32), x_t[:].rearrange("p (h d) -> p h d", h=32),
                       bc32(wf_t[:, d * 32:(d + 1) * 32]), ALU.mult, [x_b, wf_b], [xw_b])
                    acc_t, acc_b = (hf_t, hf_b) if d == 0 else (hb_t, hb_b)
                    for qd in range(4):
                        p_t, p_b = psum()
                        for j in range(2):
                            g = qd * 2 + j
                            mm(p_t[:, j * 256:(j + 1) * 256], bk_t[:, g * 128:(g + 1) * 128],
                               xw_t[:, g * 256:(g + 1) * 256], True, True, [bk_b, xw_b], [p_b])
                        tt(DVE, acc_t[:, qd * 512:(qd + 1) * 512], acc_t[:, qd * 512:(qd + 1) * 512], p_t[:],
                           ALU.add, [acc_b, p_b], [acc_b])
            ob = [Buf("o1"), Buf("o2"), Buf("o3")]
            fw.dma(SP, sbnc[:, 0:2048], hf_t[:], reads=[hf_b], writes=[sbnc_b], owner=ob[0])
            fw.dma(SP, sbnc[:, 2048:4096], hb_t[:], reads=[hb_b], writes=[sbnc_b], owner=ob[0])
            td_t, td_b = sb([128, 64], F32, "td")
            cp(DVE, td_t[:], tot_t[:, 0, :], [tot_b], [td_b])
            for c in range(1, NCH):
                tt(DVE, td_t[:], td_t[:], tot_t[:, c, :], ALU.add, [td_b, tot_b], [td_b])
            fw.dma(SP, sbnc[:, 4096:4160], td_t[:], reads=[td_b], writes=[sbnc_b], owner=ob[0])
            return

        g3_t, g3_b = sb([128, 8], F32, "g3")
        fw.dma(SP, g3_t[:], g3d, writes=[g3_b], owner=sg)
        nw_t, nw_b = sb([128, 16], F32, "nw")
        fw.dma(SP, nw_t[:], nwd, writes=[nw_b], owner=sg)
        bg_t, bg_b = sb([128, 16], F32, "bg")
        fw.dma(SP, bg_t[:], bgd, writes=[bg_b], owner=sg)
        mk_t, mk_b = sb([128, 34], F32, "mk")
        fw.dma(SP, mk_t[:], mkd, writes=[mk_b], owner=sg)

        oh_t, oh_b = sb([33, 512], F32, "oh")
        fw.dma(SP, oh_t[:], ohd, writes=[oh_b], owner=sg)
        tb_t, tb_b = sb([33, 16], F32, "tb")
        fw.dma(SP, tb_t[:], tbd, writes=[tb_b], owner=sg)
        fw.close_group(sg, [g3_b, nw_b, bg_b, mk_b, oh_b, tb_b])
        p_t, p_b = psum()
        mm(p_t[0:16, :], tb_t[:], oh_t[:], True, True, [tb_b, oh_b], [p_b])
        vr_t, vr_b = sb([16, 512], F32, "vr")
        cp(DVE, vr_t[:], p_t[0:16, :], [p_b], [vr_b])
        vrd_b = Buf("vreld")
        fw.dma(SP, vreld.ap(), vr_t[:], reads=[vr_b], writes=[vrd_b])
        bias_t, bias_b = sb([128, 16, 384], BF16, "bias")
        hk = [sb([128, 384], F32, "hk") for _ in range(1)]
        for h in range(16):
            hk_t, hk_b = hk[0]
            fw.dma(SP, hk_t[:], bass.AP(vreld, h * 512, [[1, 128], [1, 384]]), reads=[vrd_b], writes=[hk_b])
            p_t, p_b = psum()
            mm(p_t[:, 0:384], antiid, hk_t[:], True, True, [cs_b, hk_b], [p_b])
            cp(ACT, bias_t[:, h, :], p_t[:, 0:384], [p_b], [bias_b])

        hin = []
        u_t, u_b = sb([128, 2048], F32, "u")
        st_t, st_b = u_t, u_b
        dc_t, dc_b = sb([128, 512], F32, "dc8")
        fw.dma(SP, dc_t[:].rearrange("p (r c) -> p r c", r=8), sgath.rearrange("(r p) c -> p r c", p=128)[:, :, 4096:4160], reads=[sgath_b], writes=[dc_b])
        act(dc_t[:], dc_t[:], AF.Exp, [dc_b], [dc_b])
        dd_t, dd_b = sb([128, 32], F32, "dd")
        for d in range(2):
            h_t, h_b = sb([128, 2048], BF16, "hin")
            fw.op(DVE, lambda e, h_t=h_t: e.memset(h_t[:], 0.0), [], [h_b])
            order = range(8) if d == 0 else range(7, -1, -1)
            for r in order:
                mcol = mk_t[:, 2 + d * 8 + r:3 + d * 8 + r]
                fw.dma(SP, st_t[:], sgath[r * 128:(r + 1) * 128, d * 2048:(d + 1) * 2048], reads=[sgath_b], writes=[st_b], owner=st_b)
                ts(DVE, dd_t[:], dc_t[:, r * 64 + d * 32:r * 64 + d * 32 + 32], -1.0, mcol, ALU.add, ALU.mult,
                   [dc_b, mk_b], [dd_b])
                ts(DVE, dd_t[:], dd_t[:], 1.0, None, ALU.add, ALU.bypass, [dd_b], [dd_b])
                hv = h_t[:].rearrange("p (h d) -> p h d", h=32)
                tt(DVE, hv, hv, bc32(dd_t[:]), ALU.mult, [h_b, dd_b], [h_b])
                stt(DVE, h_t[:], st_t[:], mcol, h_t[:], ALU.mult, ALU.add, [st_b, mk_b, h_b], [h_b])
            hin.append((h_t, h_b))

        ep_t, ep_b = sb([128, NCH, 32], F32, "epre")
        es_t, es_b = sb([128, NCH, 32], F32, "esuf")
        fw.op(DVE, lambda e: e.memset(ep_t[:, 0, :], 0.0), [], [ep_b])
        fw.op(DVE, lambda e: e.memset(es_t[:, NCH - 1, :], 0.0), [], [es_b])
        for c in range(1, NCH):
            tt(DVE, ep_t[:, c, :], ep_t[:, c - 1, :], tot_t[:, c - 1, 0:32], ALU.add, [ep_b, tot_b], [ep_b])
        for c in range(NCH - 2, -1, -1):
            tt(DVE, es_t[:, c, :], es_t[:, c + 1, :], tot_t[:, c + 1, 32:64], ALU.add, [es_b, tot_b], [es_b])
        act(ep_t[:], ep_t[:], AF.Exp, [ep_b], [ep_b])
        act(es_t[:], es_t[:], AF.Exp, [es_b], [es_b])
        wfc_t, wfc_b = sb([128, NCH, 32], F32, "wfc")
        tt(DVE, wfc_t[:], tot_t[:, :, 0:32], cs2_t[:, :, 0:32], ALU.subtract, [tot_b, cs2_b], [wfc_b])
        act(wfc_t[:], wfc_t[:], AF.Exp, [wfc_b], [wfc_b])
        tt(DVE, wfc_t[:], wfc_t[:], dt_t[:, :, 0:32], ALU.mult, [wfc_b, dt_b], [wfc_b])

        hfl_t, hfl_b = sb([128, 2048], F32, "hfl")
        fw.op(DVE, lambda e: e.memset(hfl_t[:], 0.0), [], [hfl_b])
        hflb_t, hflb_b = sb([128, 2048], BF16, "hflb")
        fw.op(POOL, lambda e: e.memset(hflb_t[:], 0.0), [], [hflb_b])
        hind = [sb([128, 2048], BF16, "hind") for _ in range(2)]
        hblc = [sb([128, 2048], BF16, "hblc") for _ in range(1)]
        sz_t, sz_b = sb([128, 2048], BF16, "sz")
        un_t, un_b = sb([128, 2048], BF16, "un")
        uT_t, uT_b = sb([128, 16, 128], BF16, "uT")
        cbm = [sb([128, 2, 256], F32, "cbm") for _ in range(1)]
        Tm = [sb([128, 512], F32, "Tm") for _ in range(1)]
        Em = [sb([128, 512], F32, "Em") for _ in range(1)]
        Mq = [sb([128, 4, 512], BF16, "Mq") for _ in range(1)]
        t1 = [sb([128, 512], F32, "t1") for _ in range(1)]
        t2 = [sb([128, 512], F32, "t2") for _ in range(1)]
        t3 = [sb([128, 512], F32, "t3") for _ in range(1)]
        qT_t, qT_b = sb([128, 8, 128], BF16, "qT")
        kT_t, kT_b = sb([128, 4, 384], BF16, "kT")
        v_t, v_b = sb([128, 3, 256], BF16, "v")
        lg = [sb([128, 384], F32, "lg") for _ in range(1)]
        pn = [sb([128, 384], BF16, "pn") for _ in range(1)]
        pT = [sb([128, 384], BF16, "pT") for _ in range(1)]
        sm = [sb([128, 8], F32, "sm") for _ in range(2)]
        oT_t, oT_b = sb([64, 16, 128], BF16, "oT")
        mxT_t, mxT_b = sb([128, 8, 128], BF16, "mxT")
        gsb = [sb([128, 128], F32, "gs") for _ in range(2)]
        mxa = [sb([128, 128], F32, "mxa") for _ in range(2)]
        xres = xin
        mo = [sb([128, D], F32, "mo") for _ in range(1)]
        x1_b = Buf("x1d")
        U_dir = [U_incl, U_rev]
        m_dir = [maskF, maskB]

        for c in range(NCH):
            hT_t, hw_b = load_window(c)
            hbc = [hw_b]
            x_t, x_b, bk_t, bk_b, bT_t, bT_b, cT_t, cT_b = xbc_chunk(c, True, hT_t, hw_b)
            nb_t, nb_b, ee_t, ee_b, et_t, etot_b = chunk_small(c)
            for zb in range(4):
                wv, wb = load_w(w_in[:, OFF_Z + zb * 512:OFF_Z + (zb + 1) * 512], 8, 512)
                p_t, p_b = psum()
                for k in range(8):
                    mm(p_t[:], hT_t[:, k, col:col + 128], wv[:, k, :], k == 0, k == 7, hbc + [wb], [p_b])
                act(sz_t[:, zb * 512:(zb + 1) * 512], p_t[:], AF.Silu, [p_b], [sz_b])
            hdf_t, hdf_b = hind[0]
            hdb_t, hdb_b = hind[1]
            tt(DVE, hdf_t[:].rearrange("p (h d) -> p h d", h=32), hin[0][0][:].rearrange("p (h d) -> p h d", h=32),
               bc32(ep_t[:, c, :]), ALU.mult, [hin[0][1], ep_b], [hdf_b])
            tt(POOL, hdb_t[:].rearrange("p (h d) -> p h d", h=32), hin[1][0][:].rearrange("p (h d) -> p h d", h=32),
               bc32(es_t[:, c, :]), ALU.mult, [hin[1][1], es_b], [hdb_b])
            hl_t, hl_b = hblc[0]
            fw.dma(SP, hl_t[:], hbl_d[c * 128:(c + 1) * 128, :], reads=[hbld_b], writes=[hl_b], owner=hl_b)
            for q in range(4):
                qs = slice(q * 512, (q + 1) * 512)
                hs = slice(q * 8, (q + 1) * 8)
                p_t, p_b = psum()
                for j in range(2):
                    g = 2 * q + j
                    mm(p_t[:, j * 128:(j + 1) * 128], bT_t[:, g, :], cT_t[:, g, :], True, True, [bT_b, cT_b], [p_b])
                cb_t2, cb_b2 = cbm[0]
                for d in range(2):
                    tt(DVE, cb_t2[:, d, :].rearrange("p (j l) -> p j l", j=2),
                       p_t[:, 0:256].rearrange("p (j l) -> p j l", j=2),
                       m_dir[d].unsqueeze(1).to_broadcast([128, 2, 128]), ALU.mult, [p_b, cs_b], [cb_b2])
                M_t, M_b = Mq[0]
                for d in range(2):
                    for j in range(2):
                        g = 2 * q + j
                        hd0 = d * 32 + g * 4
                        p2_t, p2_b = psum()
                        for r in range(4):
                            mm(p2_t[:, r * 128:(r + 1) * 128], adt_t[:, c, hd0 + r:hd0 + r + 1].to_broadcast([128, 128]),
                               U_dir[d], True, True, [adt_b, cs_b], [p2_b])
                        T_t, T_b = Tm[0]
                        tt(DVE, T_t[:].rearrange("p (r l) -> p r l", r=4), p2_t[:].rearrange("p (r l) -> p r l", r=4),
                           cs2_t[:, c, hd0:hd0 + 4].unsqueeze(2).to_broadcast([128, 4, 128]), ALU.min,
                           [p2_b, cs2_b], [T_b])
                        E_t, E_b = Em[0]
                        for r in range(4):
                            act(E_t[:, r * 128:(r + 1) * 128], T_t[:, r * 128:(r + 1) * 128], AF.Exp, [T_b, nb_b], [E_b],
                                bias=nb_t[:, hd0 + r:hd0 + r + 1])
                        tt(POOL, M_t[:, d * 2 + j, :].rearrange("p (r l) -> p r l", r=4),
                           E_t[:].rearrange("p (r l) -> p r l", r=4),
                           cb_t2[:, d, j * 128:(j + 1) * 128].unsqueeze(1).to_broadcast([128, 4, 128]), ALU.mult,
                           [E_b, cb_b2], [M_b])
                py_t, py_b = psum()
                for j in range(2):
                    for r in range(4):
                        hl = j * 4 + r
                        h = q * 8 + hl
                        mm(py_t[:, hl * 64:(hl + 1) * 64], M_t[:, j, r * 128:(r + 1) * 128], x_t[:, h * 64:(h + 1) * 64],
                           True, False, [M_b, x_b], [py_b])
                        mm(py_t[:, hl * 64:(hl + 1) * 64], M_t[:, 2 + j, r * 128:(r + 1) * 128],
                           x_t[:, h * 64:(h + 1) * 64], False, True, [M_b, x_b], [py_b])
                pf_t, pf_b = psum()
                pb_t, pb_b = psum()
                for j in range(2):
                    g = 2 * q + j
                    gs_ = slice(g * 256, (g + 1) * 256)
                    mm(pf_t[:, j * 256:(j + 1) * 256], cT_t[:, g, :], hflb_t[:, gs_], True, False, [cT_b, hflb_b], [pf_b])
                    mm(pf_t[:, j * 256:(j + 1) * 256], cT_t[:, g, :], hdf_t[:, gs_], False, True, [cT_b, hdf_b], [pf_b])
                    mm(pb_t[:, j * 256:(j + 1) * 256], cT_t[:, g, :], hl_t[:, gs_], True, False, [cT_b, hl_b], [pb_b])
                    mm(pb_t[:, j * 256:(j + 1) * 256], cT_t[:, g, :], hdb_t[:, gs_], False, True, [cT_b, hdb_b], [pb_b])
                a1, a1b = t1[0]
                a2, a2b = t2[0]
                a3, a3b = t3[0]

                def v8(ap):
                    return ap.rearrange("p (h d) -> p h d", h=8)

                def b8(ap2d):
                    return ap2d.unsqueeze(2).to_broadcast([128, 8, 64])

                tt(DVE, v8(a1[:]), v8(pf_t[:]), b8(ee_t[:, q * 8:(q + 1) * 8]), ALU.mult, [pf_b, ee_b], [a1b])
                tt(DVE, v8(a2[:]), v8(pb_t[:]), b8(ee_t[:, 32 + q * 8:32 + (q + 1) * 8]), ALU.mult, [pb_b, ee_b], [a2b])
                tt(POOL, v8(a3[:]), v8(x_t[:, qs]), b8(dsk[:, hs]), ALU.mult, [x_b, rp_b], [a3b])
                tt(POOL, a1[:], a1[:], a2[:], ALU.add, [a1b, a2b], [a1b])
                tt(POOL, a1[:], a1[:], a3[:], ALU.add, [a1b, a3b], [a1b])
                tt(DVE, a1[:], a1[:], py_t[:], ALU.add, [a1b, py_b], [a1b])
                tt(DVE, u_t[:, qs], a1[:], sz_t[:, qs], ALU.mult, [a1b, sz_b], [u_b])
                xw_t, xw_b = xw[q % 2]
                tt(POOL, v8(xw_t[:, 0:512]), v8(x_t[:, qs]), b8(wfc_t[:, c, hs]), ALU.mult, [x_b, wfc_b], [xw_b])
                ps_t, ps_b = psum()
                for j in range(2):
                    g = 2 * q + j
                    mm(ps_t[:, j * 256:(j + 1) * 256], bk_t[:, g * 128:(g + 1) * 128], xw_t[:, j * 256:(j + 1) * 256],
                       True, True, [bk_b, xw_b], [ps_b])
                tt(DVE, v8(hfl_t[:, qs]), v8(hfl_t[:, qs]), b8(et_t[:, hs]), ALU.mult, [hfl_b, etot_b], [hfl_b])
                tt(DVE, hfl_t[:, qs], hfl_t[:, qs], ps_t[:], ALU.add, [hfl_b, ps_b], [hfl_b])
                cp(ACT, hflb_t[:, qs], hfl_t[:, qs], [hfl_b], [hflb_b])
            if DEBUG:
                fw.dma(SP, dbg_u[c * 128:(c + 1) * 128, :], u_t[:], reads=[u_b], writes=[dbg_b], owner=u_b)
            r8_t, r8_b = rms_rstd(u_t[:], u_b, 8, 256)
            tt(DVE, un_t[:].rearrange("p (g d) -> p g d", g=8), u_t[:].rearrange("p (g d) -> p g d", g=8),
               r8_t.unsqueeze(2).to_broadcast([128, 8, 256]), ALU.mult, [u_b, r8_b], [un_b])
            for hf in range(2):
                p_t, p_b = psum()
                pv = p_t[:].bitcast(BF16)
                for k in range(8):
                    i = hf * 8 + k
                    tr(pv[:, k * 128:(k + 1) * 128], un_t[:, i * 128:(i + 1) * 128], identb, [un_b, idb_b], [p_b])
                tt(DVE, uT_t[:, hf * 8:(hf + 1) * 8, :], pv.rearrange("p (k c) -> p k c", k=8),
                   nw_t[:, hf * 8:(hf + 1) * 8].unsqueeze(2).to_broadcast([128, 8, 128]), ALU.mult, [p_b, nw_b], [uT_b])

            for blk in range(2):
                wv, wb = load_w(w_in[:, OFF_Q + blk * 512:OFF_Q + (blk + 1) * 512], 8, 512)
                for j in range(4):
                    p_t, p_b = psum()
                    for k in range(8):
                        mm(p_t[:, 0:128], wv[:, k, j * 128:(j + 1) * 128], hT_t[:, k, col:col + 128], k == 0, k == 7,
                           hbc + [wb], [p_b])
                    act(qT_t[:, blk * 4 + j, :], p_t[:, 0:128], AF.Copy, [p_b], [qT_b], scale=0.125)
            wv, wb = load_w(w_in[:, OFF_K:OFF_K + 512], 8, 512)
            hb3 = [hw_b]
            kd_t, kd_b = wpool[wi[0] % NW]
            wi[0] += 1
            kdv = kd_t[:, 0:8 * 512].rearrange("p (k c) -> p k c", k=8)
            for g in range(4):
                for half in range(2):
                    fw.dma(POOL, kdv[:, :, g * 128 + half * 64:g * 128 + half * 64 + 64],
                           w_in[:, OFF_K + g * 64:OFF_K + (g + 1) * 64].rearrange("(k p) c -> p k c", p=128),
                           writes=[kd_b])
            for g in range(4):
                p_t, p_b = psum()
                for k in range(8):
                    mm(p_t[:, 0:384], kdv[:, k, g * 128:(g + 1) * 128], hT_t[:, k, col - 128:col + 256], k == 0, k == 7,
                       hb3 + [kd_b], [p_b])
                cp(ACT, kT_t[:, g, :], p_t[:, 0:384], [p_b], [kT_b])
            for jt in range(3):
                p_t, p_b = psum()
                cc = col - 128 + jt * 128
                for k in range(8):
                    mm(p_t[:, 0:256], hT_t[:, k, cc:cc + 128], wv[:, k, 256:512], k == 0, k == 7, hb3 + [wb], [p_b])
                cp(ACT, v_t[:, jt, :], p_t[:, 0:256], [p_b], [v_b])
            for hq in range(16):
                g = hq // 4
                po = (hq % 2) * 64
                p_t, p_b = psum()
                mm(p_t[:, 0:384], qT_t[po:po + 64, hq // 2, :], kT_t[po:po + 64, g, :], True, True, [qT_b, kT_b], [p_b])
                l_t, l_b = lg[0]
                tt(DVE, l_t[:], p_t[:, 0:384], bias_t[:, hq, :], ALU.add, [p_b, bias_b], [l_b])
                if c == 0:
                    ts(DVE, l_t[:, 0:128], l_t[:, 0:128], mk_t[:, 0:1], None, ALU.add, ALU.bypass, [l_b, mk_b], [l_b])
                if c == NCH - 1:
                    ts(DVE, l_t[:, 256:384], l_t[:, 256:384], mk_t[:, 1:2], None, ALU.add, ALU.bypass, [l_b, mk_b], [l_b])
                s_t, s_b = sm[hq % 2]
                fw.op(DVE, lambda e, s_t=s_t, l_t=l_t: e.reduce_max(out=s_t[:, 0:1], in_=l_t[:], axis=AX.X), [l_b], [s_b])
                tt(DVE, s_t[:, 0:1], s_t[:, 0:1], sink[:, hq:hq + 1], ALU.max, [s_b, rp_b], [s_b])
                ts(DVE, s_t[:, 1:2], s_t[:, 0:1], -1.0, None, ALU.mult, ALU.bypass, [s_b], [s_b])
                act(l_t[:], l_t[:], AF.Exp, [l_b, s_b], [l_b, s_b], bias=s_t[:, 1:2], accum_out=s_t[:, 2:3])
                act(s_t[:, 3:4], sink[:, hq:hq + 1], AF.Exp, [rp_b, s_b], [s_b], bias=s_t[:, 1:2])
                tt(DVE, s_t[:, 4:5], s_t[:, 2:3], s_t[:, 3:4], ALU.add, [s_b], [s_b])
                recip(s_t[:, 5:6], s_t[:, 4:5], [s_b], [s_b])
                n_t, n_b = pn[0]
                ts(POOL, n_t[:], l_t[:], s_t[:, 5:6], None, ALU.mult, ALU.bypass, [l_b, s_b], [n_b])
                p2_t, p2_b = psum()
                pv = p2_t[:].bitcast(BF16)
                for jt in range(3):
                    tr(pv[:, jt * 128:(jt + 1) * 128], n_t[:, jt * 128:(jt + 1) * 128], identb, [n_b, idb_b], [p2_b])
                pt_t, pt_b = pT[0]
                cp(ACT, pt_t[:], pv[:, 0:384], [p2_b], [pt_b])
                p3_t, p3_b = psum()
                for jt in range(3):
                    mm(p3_t[0:64, 0:128], v_t[:, jt, g * 64:(g + 1) * 64], pt_t[:, jt * 128:(jt + 1) * 128], jt == 0,
                       jt == 2, [v_b, pt_b], [p3_b])
                cp(ACT, oT_t[:, hq, :], p3_t[0:64, 0:128], [p3_b], [oT_b])

            if DEBUG:
                fw.dma(SP, dbg_o[c * 64:(c + 1) * 64, :], oT_t[:].rearrange("p h c -> p (h c)"), reads=[oT_b], writes=[dbg_b], owner=oT_b)
            for blk in range(4):
                cs_ = slice(blk * 256, (blk + 1) * 256)
                wso_v, wso_b = load_w(w_so[:, cs_], 16, 256)
                wgs_v, wgs_b = load_w(w_in[:, OFF_G + blk * 256:OFF_G + (blk + 1) * 256], 8, 256)
                for j in range(2):
                    i = blk * 2 + j
                    js = slice(j * 128, (j + 1) * 128)
                    pa_t, pa_b = psum()
                    for k in range(16):
                        mm(pa_t[:, 0:128], wso_v[:, k, js], uT_t[:, k, :], k == 0, k == 15, [wso_b, uT_b], [pa_b])
                    pg_t, pg_b = psum()
                    for k in range(8):
                        mm(pg_t[:, 0:128], wgs_v[:, k, js], hT_t[:, k, col:col + 128], k == 0, k == 7, hbc + [wgs_b], [pg_b])
                    g_t, g_b = gsb[0]
                    act(g_t[:], pg_t[:, 0:128], AF.Sigmoid, [pg_b, bg_b], [g_b], bias=bg_t[:, i:i + 1])
                    m_t, m_b = mxa[j]
                    tt(DVE, m_t[:], g_t[:], pa_t[:, 0:128], ALU.mult, [g_b, pa_b], [m_b])
                wao_t, wao_b = wpool[wi[0] % NW]
                wi[0] += 1
                wao_v = wao_t[0:64, 0:16 * 256].rearrange("p (h c) -> p h c", h=16)
                fw.dma(POOL, wao_v, w_ao[:, cs_].rearrange("(h p) c -> p h c", p=64), writes=[wao_b])
                wga_v, wga_b = load_w(w_in[:, OFF_G + 1024 + blk * 256:OFF_G + 1024 + (blk + 1) * 256], 8, 256)
                for j in range(2):
                    i = blk * 2 + j
                    js = slice(j * 128, (j + 1) * 128)
                    m_t, m_b = mxa[j]
                    pc_t, pc_b = psum()
                    for h in range(16):
                        mm(pc_t[:, 0:128], wao_v[:, h, js], oT_t[:, h, :], h == 0, h == 15, [wao_b, oT_b], [pc_b])
                    pd_t, pd_b = psum()
                    for k in range(8):
                        mm(pd_t[:, 0:128], wga_v[:, k, js], hT_t[:, k, col:col + 128], k == 0, k == 7, hbc + [wga_b], [pd_b])
                    g2_t, g2_b = gsb[1]
                    act(g2_t[:], pd_t[:, 0:128], AF.Sigmoid, [pd_b, bg_b], [g2_b], bias=bg_t[:, 8 + i:9 + i])
                    tt(DVE, g2_t[:], g2_t[:], pc_t[:, 0:128], ALU.mult, [g2_b, pc_b], [g2_b])
                    tt(DVE, mxT_t[:, i, :], m_t[:], g2_t[:], ALU.add, [m_b, g2_b], [mxT_b])
            if DEBUG:
                fw.dma(SP, dbg_m[c * 128:(c + 1) * 128, :], mxT_t[:].rearrange("p h c -> p (h c)"), reads=[mxT_b], writes=[dbg_b], owner=mxT_b)
            mo_t, mo_b = mo[0]
            for half in range(2):
                wv, wb = load_w(w_o[:, half * 512:(half + 1) * 512], 8, 512)
                p_t, p_b = psum()
                for k in range(8):
                    mm(p_t[:], mxT_t[:, k, :], wv[:, k, :], k == 0, k == 7, [mxT_b, wb], [p_b])
                cp(ACT, mo_t[:, half * 512:(half + 1) * 512], p_t[:], [p_b], [mo_b])
            r_t, r_b = rms_rstd(mo_t[:], mo_b, 1, D)
            xr_t, xr_b = xres[c % 2]
            fw.dma(SP, xr_t[:], xh[128 + c * 128:256 + c * 128, :], reads=xh_rd, writes=[xr_b], owner=xr_b)
            stt(DVE, mo_t[:], mo_t[:], r_t[:, 0:1], pmix, ALU.mult, ALU.mult, [mo_b, r_b, rp_b], [mo_b])
            tt(DVE, xr_t[:], xr_t[:], mo_t[:], ALU.add, [xr_b, mo_b], [xr_b])
            fw.dma(SP, x1d[c * 128:(c + 1) * 128, :], xr_t[:], reads=[xr_b], writes=[x1_b], owner=xr_b)

        fw.barrier()
        scope[0].close()
        y_b = yout_b
        g3_t, g3_b = sb([128, 8], F32, "g3m")
        fw.dma(SP, g3_t[:], g3d, writes=[g3_b], owner=sg)
        pm_t, pm_b = sb([128, D], F32, "pmlp")
        fw.dma(SP, pm_t[:], rowp[:, 1200:2224].partition_broadcast(128).rearrange("p o c -> p (o c)"), writes=[pm_b], owner=sg)
        fw.close_group(sg, [g3_b, pm_b])
        xld, xst = Buf("xld"), Buf("xst")
        hb2.clear()
        hb2.append(sb([128, D], BF16, "hbm"))
        xg = [sb([128, D], F32, "xg") for _ in range(4)]
        mog = [sb([128, D], F32, "mog") for _ in range(4)]
        h2_t, h2_b = sb([128, 8, 512], BF16, "h2T")
        aT_t, aT_b = sb([128, 32, 512], BF16, "aT")
        rl = [sb([128, 512], F32, "rl") for _ in range(2)]
        for grp in range(NCH // 4):
            for t4 in range(4):
                c = grp * 4 + t4
                xr_t, xr_b = xg[t4]
                fw.dma(SP, xr_t[:], x1d[c * 128:(c + 1) * 128, :], reads=[x1_b], writes=[xr_b], owner=xld)
            fw.close_group(xld, [b for _, b in xg])
            for t4 in range(4):
                xr_t, xr_b = xg[t4]
                norm_transpose(xr_t[:], xr_b, g3_t[:], g3_b, h2_t[:, :, t4 * 128:(t4 + 1) * 128], h2_b)
            for blk in range(8):
                wv, wb = load_w(w1[:, blk * 512:(blk + 1) * 512], 8, 512)
                for j in range(4):
                    p_t, p_b = psum()
                    for k in range(8):
                        mm(p_t[:], wv[:, k, j * 128:(j + 1) * 128], h2_t[:, k, :], k == 0, k == 7, [wb, h2_b], [p_b])
                    r_t2, r_b2 = rl[j % 2]
                    act(r_t2[:], p_t[:], AF.Relu, [p_b], [r_b2])
                    tt(DVE if j % 2 == 0 else POOL, aT_t[:, blk * 4 + j, :], r_t2[:], r_t2[:], ALU.mult, [r_b2], [aT_b])
            for cb in range(2):
                acc = [psum() for _ in range(4)]
                for blk in range(8):
                    wv, wb = load_w(w2[blk * 512:(blk + 1) * 512, cb * 512:(cb + 1) * 512], 4, 512)
                    for f in range(4):
                        ff = blk * 4 + f
                        for t4 in range(4):
                            mm(acc[t4][0][:], aT_t[:, ff, t4 * 128:(t4 + 1) * 128], wv[:, f, :], ff == 0, ff == 31,
                               [aT_b, wb], [acc[t4][1]], inc=(ff == 31 or (f == 3 and t4 == 3)))
                for t4 in range(4):
                    cp(ACT, mog[t4][0][:, cb * 512:(cb + 1) * 512], acc[t4][0][:], [acc[t4][1]], [mog[t4][1]])
            for t4 in range(4):
                c = grp * 4 + t4
                xr_t, xr_b = xg[t4]
                mo_t, mo_b = mog[t4]
                r_t, r_b = rms_rstd(mo_t[:], mo_b, 1, D)
                stt(DVE, mo_t[:], mo_t[:], r_t[:, 0:1], pm_t[:], ALU.mult, ALU.mult, [mo_b, r_b, pm_b], [mo_b])
                tt(DVE, xr_t[:], xr_t[:], mo_t[:], ALU.add, [xr_b, mo_b], [xr_b])
                if l == 0:
                    fw.dma(SP, xh1[128 + c * 128:256 + c * 128, :], xr_t[:], reads=[xr_b], writes=[xh1_b], owner=xst)
                    if c == 0:
                        fw.dma(SP, xbn[0:128, :], xr_t[:], reads=[xr_b], writes=[xbn_b, xh1_b], owner=xst)
                    if c == NCH - 1:
                        fw.dma(SP, xbn[128:256, :], xr_t[:], reads=[xr_b], writes=[xbn_b, xh1_b], owner=xst)
                else:
                    fw.dma(SP, yout[c * 128:(c + 1) * 128, :], xr_t[:], reads=[xr_b], writes=[y_b], owner=xst)
        return

    for l in range(2):
        with ExitStack() as st:
            scope[0] = st
            emit(1, l)
            fw.barrier()
        scope[0] = None
        fw.collective("AllGather", ALU.bypass, sbnc, sgath, cdm_t[:], [sbnc_b], [sgath_b, cdm_b])
        fw.barrier()
        with ExitStack() as st:
            scope[0] = st
            emit(2, l)
            fw.barrier()
        scope[0] = None
        if l == 0:
            fw.collective("AllGather", ALU.bypass, xbn, xgt, cdm_t[:], [xbn_b], [xgt_b, cdm_b])
            fw.barrier()
            with ExitStack() as st:
                scope[0] = st
                mk_t, mk_b = sb([128, 34], F32, "mk")
                fw.dma(SP, mk_t[:], mkd, writes=[mk_b])
                acc = [sb([128, D], F32, "hacc") for _ in range(2)]
                ld = [sb([128, D], F32, "hld") for _ in range(2)]
                for side in range(2):
                    a_t, a_b = acc[side]
                    fw.op(DVE, lambda e, a_t=a_t: e.memset(a_t[:], 0.0), [], [a_b])
                    for r in range(8):
                        l_t, l_b = ld[r % 2]
                        row0 = r * 256 + (128 if side == 0 else 0)
                        fw.dma(SP, l_t[:], xgt[row0:row0 + 128, :], reads=[xgt_b], writes=[l_b], owner=l_b)
                        mcol = mk_t[:, 18 + side * 8 + r:19 + side * 8 + r]
                        fw.op(DVE, lambda e, a_t=a_t, l_t=l_t, mcol=mcol: e.scalar_tensor_tensor(
                            out=a_t[:], in0=l_t[:], scalar=mcol, in1=a_t[:], op0=ALU.mult, op1=ALU.add),
                            [l_b, mk_b, a_b], [a_b])
                    dst = xh1[0:128, :] if side == 0 else xh1[17 * 128:18 * 128, :]
                    fw.dma(SP, dst, a_t[:], reads=[a_b], writes=[xh1_b], owner=a_b)
                fw.barrier()
            scope[0] = None
    endt, endb = sb([128, 1], F32, "end")
    fw.op(POOL, lambda e: e.memset(endt[:], 0.0), [], [endb])
    fw.op(ACT, lambda e: e.copy(out=endt[:], in_=endt[:]), [endb], [endb])
    fw.op(DVE, lambda e: e.tensor_copy(out=endt[:], in_=endt[:]), [endb], [endb])
    fw.finish([yout_b, endb])
    return nc


def _t5_bucket(rel):
    nb = 16
    max_exact = 8
    ret = np.where(rel > 0, nb, 0)
    n = np.abs(rel)
    nf = np.maximum(n, 1).astype(np.float32)
    large = max_exact + (np.log(nf / max_exact) / math.log(128 / max_exact) * (nb - max_exact)).astype(np.int32)
    large = np.minimum(large, nb - 1)
    return ret + np.where(n < max_exact, n, large)


def _consts():
    I = np.eye(128, dtype=np.float32)
    tri = np.triu(np.ones((128, 128), np.float32))
    cst = np.concatenate([I, I[::-1], tri, tri.T, tri, tri.T, np.ones((128, 128), np.float32)], axis=1)
    rel = np.arange(511) - 255
    bk = _t5_bucket(rel)
    oh = np.zeros((33, 512), np.float32)
    for idx in range(511):
        if abs(rel[idx]) <= 128:
            oh[bk[idx], idx] = 1.0
        else:
            oh[32, idx] = 1.0
    oh[32, 511] = 1.0
    return np.ascontiguousarray(cst), oh


_PROG = []


def kernel(x, pre_mix_norm, w_in, b_gate, conv_w, conv_b, dt_bias, a_log, d_skip, ssd_norm,
           w_ssd_out, attn_sink, rel_bias_table, w_attn_out, w_o, post_mix_norm,
           pre_mlp_norm, w_mlp_in, w_mlp_out, post_mlp_norm):
    f = lambda a: np.ascontiguousarray(np.asarray(a), dtype=np.float32)
    x = f(x)
    cst, oh = _consts()
    tb = np.concatenate([f(rel_bias_table), np.full((1, 16), NEG, np.float32)], axis=0)
    p128 = lambda v, k: f(v).reshape(k, 128).T
    st2 = lambda fn: np.ascontiguousarray(np.concatenate([fn(0), fn(1)], axis=0))
    rowp = st2(lambda l: np.concatenate([f(dt_bias[l]).reshape(-1), f(a_log[l]).reshape(-1), f(d_skip[l]).reshape(-1),
                                         f(attn_sink[l]).reshape(-1), f(post_mix_norm[l]), f(post_mlp_norm[l])])[None, :])
    cw = st2(lambda l: f(conv_w[l])[:, 0, :].reshape(5, 32, 128).transpose(2, 1, 0).reshape(128, 160))
    common = {
        "w_in": f(w_in).reshape(2 * D, NIN), "cw": cw, "cb": st2(lambda l: p128(conv_b[l], 32)),
        "g1": st2(lambda l: p128(pre_mix_norm[l], 8)), "rowp": rowp, "cst": cst,
        "w_so": f(w_ssd_out).reshape(2 * 2048, D), "w_ao": f(w_attn_out).reshape(2 * D, D),
        "w_o": f(w_o).reshape(2 * D, D), "w1": f(w_mlp_in).reshape(2 * D, 4096),
        "w2": f(w_mlp_out).reshape(2 * 4096, D), "g3": st2(lambda l: p128(pre_mlp_norm[l], 8)),
        "nw": st2(lambda l: p128(ssd_norm[l], 16)), "bg": st2(lambda l: p128(b_gate[l], 16)), "oh": oh, "tb": tb,
    }
    xp = np.pad(x, ((0, 0), (128, 128), (0, 0)))
    ins = []
    for c in range(8):
        b, pos = c // 4, c % 4
        mk = np.zeros((128, 34), np.float32)
        mk[:, 0] = NEG if pos == 0 else 0.0
        mk[:, 1] = NEG if pos == 3 else 0.0
        for r in range(8):
            same = (r // 4 == b)
            mk[:, 2 + r] = 1.0 if (same and r < c) else 0.0
            mk[:, 10 + r] = 1.0 if (same and r > c) else 0.0
            mk[:, 18 + r] = 1.0 if (same and r == c - 1) else 0.0
            mk[:, 26 + r] = 1.0 if (same and r == c + 1) else 0.0
        ins.append(dict(common, xh=np.ascontiguousarray(xp[b, pos * 2048:pos * 2048 + 2304]), mk=mk))
    if not _PROG:
        _PROG.append(build_fused())
    res = run_bass_kernel_spmd(_PROG[0], ins, core_ids=list(range(8))).results
    out = np.zeros_like(x)
    for c in range(8):
        b, pos = c // 4, c % 4
        out[b, pos * 2048:(pos + 1) * 2048] = res[c]["y"]
    return out.astype(np.float32)
```

```python
import math
from contextlib import ExitStack
import numpy as np
import concourse.bass as bass
import concourse.mybir as mybir
from concourse.bass_utils import run_bass_kernel_spmd

F32 = mybir.dt.float32
BF16 = mybir.dt.bfloat16
AF = mybir.ActivationFunctionType
ALU = mybir.AluOpType
AX = mybir.AxisListType
PE, ACT, DVE, POOL, SP = range(5)
SAME_ENGINE_SYNC = True
DEBUG = False

D = 1024
NIN = 9792
OFF_Z, OFF_X, OFF_B, OFF_C, OFF_DT, OFF_Q, OFF_K, OFF_V, OFF_G = 0, 2048, 4096, 5120, 6144, 6208, 7232, 7488, 7744
NEG = -30000.0
EPS = 1e-6
NCH = 16


class Buf:
    __slots__ = ("name", "wr", "rd", "dsem", "dcnt")

    def __init__(self, name=""):
        self.name = name
        self.wr = None
        self.rd = {}
        self.dsem = None
        self.dcnt = 0


class FW:
    def __init__(self, nc):
        self.nc = nc
        self.q = [[] for _ in range(5)]
        self.esem = [nc.alloc_semaphore(name=f"es{i}") if i < 4 else None for i in range(5)]
        self.ecnt = [0] * 5
        self.waited = [dict() for _ in range(5)]
        self.nsem = 5
        self.owners = []
        self.ccsem = None
        self.ccn = 0

    def _collect(self, e, reads, writes):
        waits = {}
        wt = self.waited[e]

        def need(ev, same_ok):
            if ev is None:
                return
            sem, val, src = ev
            if src == e and same_ok:
                return
            k = id(sem)
            if wt.get(k, 0) >= val:
                return
            if k not in waits or waits[k][1] < val:
                waits[k] = (sem, val)

        raw_same_ok = (e == PE) or (not SAME_ENGINE_SYNC)
        for b in reads:
            need(b.wr, raw_same_ok)
        for b in writes:
            need(b.wr, raw_same_ok)
            for ev in b.rd.values():
                need(ev, True)
        for k, (sem, val) in waits.items():
            wt[k] = val
        return list(waits.values())

    def op(self, e, fn, reads=(), writes=(), inc=True):
        wl = self._collect(e, reads, writes)
        val = self.ecnt[e] + 1
        if inc:
            self.ecnt[e] += 1
        sem_e = self.esem[e]
        ev = (sem_e, val, e)
        for b in reads:
            b.rd[e] = ev
        for b in writes:
            b.wr = ev
            b.rd = {}

        def emit(eng):
            for sem, v in wl:
                eng.wait_ge(sem, v)
            ins = fn(eng)
            if inc:
                ins.then_inc(sem_e, 1)

        self.q[e].append(emit)

    def dma(self, e, out, in_, reads=(), writes=(), owner=None):
        wl = self._collect(e, reads, writes)
        if owner is None:
            owner = writes[0] if writes else reads[0]
        if owner.dsem is None:
            owner.dsem = self.nc.alloc_semaphore(name=f"ds{self.nsem}")
            self.nsem += 1
        if owner.dcnt == 0:
            self.owners.append(owner)
        owner.dcnt += 16
        sem = owner.dsem
        ev = (sem, owner.dcnt, -1)
        for b in reads:
            b.rd[("d", id(sem))] = ev
        for b in writes:
            b.wr = ev
            b.rd = {}

        def emit(eng):
            for s_, v in wl:
                eng.wait_ge(s_, v)
            eng.dma_start(out=out, in_=in_).then_inc(sem, 16)

        self.q[e].append(emit)

    def close_group(self, grp, bufs):
        for b in bufs:
            b.wr = (grp.dsem, grp.dcnt, -1)

    def collective(self, kind, alu, in_ap, out_ap, dummy_ap, reads, writes):
        e = POOL
        wl = self._collect(e, reads, writes)
        if self.ccsem is None:
            self.ccsem = self.nc.alloc_semaphore(name="ccsem")
        self.ccn += 1
        n = self.ccn
        ccsem = self.ccsem
        val = self.ecnt[e] + 1
        self.ecnt[e] += 1
        sem_e = self.esem[e]
        ev = (sem_e, val, e)
        for b in reads:
            b.rd[e] = ev
        for b in writes:
            b.wr = ev
            b.rd = {}

        def emit(eng):
            for s_, v in wl:
                eng.wait_ge(s_, v)
            eng.collective_compute(kind, alu, replica_groups=[list(range(8))], ins=[in_ap], outs=[out_ap]).then_inc(ccsem)
            eng.wait_ge(ccsem, n)
            eng.memset(dummy_ap, 0.0).then_inc(sem_e, 1)

        self.q[e].append(emit)

    def barrier(self):
        targets = [(self.esem[e2], self.ecnt[e2], e2) for e2 in range(5) if self.ecnt[e2] > 0]
        targets += [(b.dsem, b.dcnt, -1) for b in self.owners]
        for e in range(5):
            wl = []
            for sem, v, src in targets:
                if src == e:
                    continue
                k = id(sem)
                if self.waited[e].get(k, 0) < v:
                    self.waited[e][k] = v
                    wl.append((sem, v))

            def emit(eng, wl=wl):
                for s_, v in wl:
                    eng.wait_ge(s_, v)

            self.q[e].append(emit)
        if max(self.ecnt) > 1000:
            self.gen = getattr(self, "gen", 0) + 1
            self.esem = [self.nc.alloc_semaphore(name=f"es{i}g{self.gen}") if i < 4 else None for i in range(5)]
            self.ecnt = [0] * 5
            keep = []
            for b in self.owners:
                if b.dcnt > 1000:
                    b.dsem = None
                    b.dcnt = 0
                else:
                    keep.append(b)
            self.owners = keep

    def finish(self, final_bufs):
        nc = self.nc
        fin = []
        for b in final_bufs:
            for ev in [b.wr] + list(b.rd.values()):
                if ev is not None:
                    fin.append((ev[0], ev[1]))
        q = self.q
        with nc.Block() as block:
            @block.tensor
            def _(eng):
                for f in q[PE]:
                    f(eng)

            @block.scalar
            def _(eng):
                for f in q[ACT]:
                    f(eng)

            @block.vector
            def _(eng):
                for f in q[DVE]:
                    f(eng)

            @block.gpsimd
            def _(eng):
                for f in q[POOL]:
                    f(eng)

            @block.sync
            def _(eng):
                for f in q[SP]:
                    f(eng)
                for sem, v in fin:
                    eng.wait_ge(sem, v)


def build_fused():
    nc = bass.Bass("TRN2", target_bir_lowering=False)
    fw = FW(nc)

    def din(name, shape, dt=F32):
        return nc.dram_tensor(name, shape, dt, kind="ExternalInput").ap()

    def dout(name, shape, dt=F32):
        return nc.dram_tensor(name, shape, dt, kind="ExternalOutput").ap()

    xh0 = din("xh", [18 * 128, D])
    w_in_a = din("w_in", [2 * D, NIN])
    cw_a = din("cw", [2 * 128, 32 * 5])
    cb_a = din("cb", [2 * 128, 32])
    g1_a = din("g1", [2 * 128, 8])
    rowp_a = din("rowp", [2, 2224])
    cst = din("cst", [128, 7 * 128])
    w_so_a = din("w_so", [2 * 2048, D])
    w_ao_a = din("w_ao", [2 * D, D])
    w_o_a = din("w_o", [2 * D, D])
    w1_a = din("w1", [2 * D, 4096])
    w2_a = din("w2", [2 * 4096, D])
    g3_a = din("g3", [2 * 128, 8])
    nw_a = din("nw", [2 * 128, 16])
    bg_a = din("bg", [2 * 128, 16])
    ohd = din("oh", [33, 512])
    tbd = din("tb", [33, 16])
    mkd = din("mk", [128, 34])
    yout = dout("y", [2048, D])
    xh1 = nc.dram_tensor("xh1", [18 * 128, D], F32).ap()
    x1d = nc.dram_tensor("x1d", [2048, D], F32).ap()
    vreld = nc.dram_tensor("vreld", [16, 512], F32)
    hTd = nc.dram_tensor("hTd", [128, 8, 2304], BF16).ap()
    hbl_d = nc.dram_tensor("hbl_d", [NCH * 128, 2048], BF16).ap()
    sbnc = nc.dram_tensor("sbnc", [128, 4160], F32).ap()
    sgath = nc.dram_tensor("sgath", [8 * 128, 4160], F32).ap()
    xbn = nc.dram_tensor("xbn", [256, D], F32).ap()
    xgt = nc.dram_tensor("xgt", [8 * 256, D], F32).ap()
    sbnc_b, sgath_b, hbld_b, xh1_b, xbn_b, xgt_b = Buf("sbnc"), Buf("sgath"), Buf("hbld"), Buf("xh1"), Buf("xbn"), Buf("xgt")
    DEBUG = False
    cnt = [0]

    scope = [None]

    def sb(shape, dt, name=None):
        cnt[0] += 1
        nm = f"s_{name or 't'}{cnt[0]}"
        if scope[0] is not None:
            return scope[0].enter_context(nc.sbuf_tensor(nm, shape, dt)), Buf(nm)
        return nc.alloc_sbuf_tensor(nm, shape, dt), Buf(nm)

    def mm(out, lhsT, rhs, start, stop, reads, writes, inc=None):
        fw.op(PE, lambda e: e.matmul(out, lhsT=lhsT, rhs=rhs, start=start, stop=stop), reads, writes,
              inc=stop if inc is None else inc)

    def tr(out, in_, ident, reads, writes):
        fw.op(PE, lambda e: e.transpose(out=out, in_=in_, identity=ident), reads, writes)

    def act(out, in_, func, reads, writes, **kw):
        fw.op(ACT, lambda e: e.activation(out=out, in_=in_, func=func, **kw), reads, writes)

    def tt(eng, out, in0, in1, op, reads, writes):
        fw.op(eng, lambda e: e.tensor_tensor(out=out, in0=in0, in1=in1, op=op), reads, writes)

    def ts(eng, out, in0, s1, s2, op0, op1, reads, writes):
        if s2 is None:
            fw.op(eng, lambda e: e.tensor_scalar(out=out, in0=in0, scalar1=s1, scalar2=None, op0=op0), reads, writes)
        else:
            fw.op(eng, lambda e: e.tensor_scalar(out=out, in0=in0, scalar1=s1, scalar2=s2, op0=op0, op1=op1), reads, writes)

    def stt(eng, out, in0, scalar, in1, op0, op1, reads, writes):
        fw.op(eng, lambda e: e.scalar_tensor_tensor(out=out, in0=in0, scalar=scalar, in1=in1, op0=op0, op1=op1),
              reads, writes)

    def cp(eng, out, in_, reads, writes):
        if eng == ACT:
            fw.op(ACT, lambda e: e.copy(out=out, in_=in_), reads, writes)
        else:
            fw.op(eng, lambda e: e.tensor_copy(out=out, in_=in_), reads, writes)

    def recip(out, in_, reads, writes):
        fw.op(DVE, lambda e: e.reciprocal(out=out, in_=in_), reads, writes)

    pst = [nc.alloc_psum_tensor(f"ps{i}", [128, 512], F32) for i in range(8)]
    psb = [Buf(f"ps{i}") for i in range(8)]
    psi = [0]

    def psum():
        i = psi[0] % 8
        psi[0] += 1
        return pst[i], psb[i]

    NW = 2
    wpool = [sb([128, 4096], BF16, "w") for _ in range(NW)]
    wi = [0]

    def load_w(src, k, c):
        t, b = wpool[wi[0] % NW]
        wi[0] += 1
        view = t[:, 0:k * c].rearrange("p (k c) -> p k c", k=k)
        fw.dma(POOL, view, src.rearrange("(k p) c -> p k c", p=128), writes=[b])
        return view, b

    cs_t, cs_b = sb([128, 7 * 128], F32, "cst")
    fw.dma(SP, cs_t[:], cst, writes=[cs_b])
    ident32 = cs_t[:, 0:128]
    antiid = cs_t[:, 128:256]
    U_incl = cs_t[:, 256:384]
    U_rev = cs_t[:, 384:512]
    maskF = cs_t[:, 512:640]
    maskB = cs_t[:, 640:768]
    ones32 = cs_t[:, 768:896]
    idb_t, idb_b = sb([128, 128], BF16, "idb")
    fw.dma(POOL, idb_t[:], cst[:, 0:128], writes=[idb_b])
    identb = idb_t[:]
    junk_t, junk_b = sb([128, 1024], BF16, "junk")

    sspool = [sb([128, 8], F32, "ss") for _ in range(4)]
    ssi = [0]

    def rms_rstd(src, src_b, n, width, eps=EPS):
        ss_full, ss_b = sspool[ssi[0] % 4]
        ssi[0] += 1
        ss_t = ss_full[:, 0:n]
        for g in range(n):
            act(junk_t[:, 0:width], src[:, g * width:(g + 1) * width], AF.Square, [src_b], [junk_b, ss_b],
                accum_out=ss_t[:, g:g + 1])
        act(ss_t, ss_t, AF.Sqrt, [ss_b], [ss_b], bias=eps, scale=1.0 / width)
        recip(ss_t, ss_t, [ss_b], [ss_b])
        return ss_t, ss_b

    yout_b = Buf("yout")
    cdm_t, cdm_b = sb([128, 1], F32, "cdm")

    def emit(phase, l):
        xh = xh0 if l == 0 else xh1
        xh_rd = [] if l == 0 else [xh1_b]
        w_in = w_in_a[l * D:(l + 1) * D, :]
        cwd = cw_a[l * 128:(l + 1) * 128, :]
        cbd = cb_a[l * 128:(l + 1) * 128, :]
        g1d = g1_a[l * 128:(l + 1) * 128, :]
        rowp = rowp_a[l:l + 1, :]
        w_so = w_so_a[l * 2048:(l + 1) * 2048, :]
        w_ao = w_ao_a[l * D:(l + 1) * D, :]
        w_o = w_o_a[l * D:(l + 1) * D, :]
        w1 = w1_a[l * D:(l + 1) * D, :]
        w2 = w2_a[l * 4096:(l + 1) * 4096, :]
        g3d = g3_a[l * 128:(l + 1) * 128, :]
        nwd = nw_a[l * 128:(l + 1) * 128, :]
        bgd = bg_a[l * 128:(l + 1) * 128, :]
        hTd_b = Buf("hTd")
        dbg_b = Buf("dbg")
        rp_t, rp_b = sb([128, 2224], F32, "rowp")
        sg = Buf("setupgrp")
        fw.dma(SP, rp_t[:], rowp.partition_broadcast(128).rearrange("p o c -> p (o c)"), writes=[rp_b], owner=sg)
        dtb = rp_t[:, 0:64]
        alog = rp_t[:, 64:128]
        dsk = rp_t[:, 128:160]
        sink = rp_t[:, 160:176]
        pmix = rp_t[:, 176:1200]
        pmlp = rp_t[:, 1200:2224]
        cw_t, cw_b = sb([128, 160], F32, "cw")
        fw.dma(SP, cw_t[:], cwd, writes=[cw_b], owner=sg)
        cb_t, cb_b = sb([128, 32], F32, "cb")
        fw.dma(SP, cb_t[:], cbd, writes=[cb_b], owner=sg)
        g1_t, g1_b = sb([128, 8], F32, "g1")
        fw.dma(SP, g1_t[:], g1d, writes=[g1_b], owner=sg)
        fw.close_group(sg, [rp_b, cw_b, cb_b, g1_b])
        a_t, a_b = sb([128, 64], F32, "a")
        act(a_t[:], alog, AF.Exp, [rp_b], [a_b])
        ts(DVE, a_t[:], a_t[:], -1.0, None, ALU.mult, ALU.bypass, [a_b], [a_b])


        htile = [sb([128, 8, 128], BF16, "htile") for _ in range(2)]
        xin = [sb([128, D], F32, "xin") for _ in range(2)]
        hb2 = [sb([128, D], BF16, "hb") for _ in range(1)]

        def norm_transpose(src, src_b, gain, gain_b, dst, dst_b):
            r_t, r_b = rms_rstd(src, src_b, 1, D)
            h_t, h_b = hb2[0]
            act(h_t[:], src, AF.Copy, [src_b, r_b], [h_b], scale=r_t[:, 0:1])
            p_t, p_b = psum()
            pv = p_t[:].bitcast(BF16)
            for k in range(8):
                tr(pv[:, k * 128:(k + 1) * 128], h_t[:, k * 128:(k + 1) * 128], identb, [h_b, idb_b], [p_b])
            tt(DVE, dst, pv.rearrange("p (k c) -> p k c", k=8), gain.unsqueeze(2).to_broadcast([128, 8, 128]),
               ALU.mult, [p_b, gain_b], [dst_b])

        def small(n=64):
            return sb([128, NCH, n], F32, "dtq")

        dt_t, dt_b = small()
        adt_t, adt_b = small()
        cs2_t, cs2_b = small()
        tot_t, tot_b = small()
        nbq = [sb([128, 64], F32, "nb") for _ in range(2)]
        eeq = [sb([128, 64], F32, "ee") for _ in range(2)]
        etq = [sb([128, 64], F32, "et") for _ in range(2)]

        def chunk_small(c):
            nb_t, nb_b = nbq[c % 2]
            ee_t, ee_b = eeq[c % 2]
            et_t, et_b = etq[c % 2]
            act(nb_t[:], dt_t[:, c, :], AF.Ln, [dt_b], [nb_b])
            tt(DVE, nb_t[:], nb_t[:], cs2_t[:, c, :], ALU.subtract, [nb_b, cs2_b], [nb_b])
            act(ee_t[:], cs2_t[:, c, :], AF.Exp, [cs2_b], [ee_b])
            act(et_t[:], tot_t[:, c, :], AF.Exp, [tot_b], [et_b])
            return nb_t, nb_b, ee_t, ee_b, et_t, et_b

        wdt_v, wdt_b = load_w(w_in[:, OFF_DT:OFF_DT + 64], 8, 64)
        for t in range(18):
            x_t, x_b = xin[t % 2]
            fw.dma(SP, x_t[:], xh[t * 128:(t + 1) * 128, :], reads=xh_rd, writes=[x_b], owner=x_b)
            ht_t, ht_b = htile[t % 2]
            norm_transpose(x_t[:], x_b, g1_t[:], g1_b, ht_t[:], ht_b)
            fw.dma(SP, hTd[:, :, t * 128:(t + 1) * 128], ht_t[:], reads=[ht_b], writes=[hTd_b], owner=ht_b)
            if 1 <= t <= 16:
                c = t - 1
                p_t, p_b = psum()
                for k in range(8):
                    mm(p_t[:, 0:64], ht_t[:, k, :], wdt_v[:, k, :], k == 0, k == 7, [ht_b, wdt_b], [p_b])
                tt(DVE, dt_t[:, c, :], p_t[:, 0:64], dtb, ALU.add, [p_b, rp_b], [dt_b])
                act(dt_t[:, c, :], dt_t[:, c, :], AF.Exp, [dt_b], [dt_b])
                act(dt_t[:, c, :], dt_t[:, c, :], AF.Ln, [dt_b], [dt_b], bias=1.0)
                tt(DVE, adt_t[:, c, :], dt_t[:, c, :], a_t[:], ALU.mult, [dt_b, a_b], [adt_b])
                p2_t, p2_b = psum()
                mm(p2_t[:, 0:32], U_incl, adt_t[:, c, 0:32], True, True, [cs_b, adt_b], [p2_b])
                mm(p2_t[:, 32:64], U_rev, adt_t[:, c, 32:64], True, True, [cs_b, adt_b], [p2_b])
                mm(p2_t[:, 64:128], ones32, adt_t[:, c, :], True, True, [cs_b, adt_b], [p2_b])
                cp(DVE, cs2_t[:, c, :], p2_t[:, 0:64], [p2_b], [cs2_b])
                cp(DVE, tot_t[:, c, :], p2_t[:, 64:128], [p2_b], [tot_b])

        hwin = [sb([128, 8, 384], BF16, "hwin") for _ in range(2)]
        col = 128

        def load_window(c):
            hw_t, hw_b = hwin[c % 2]
            fw.dma(SP, hw_t[:], hTd[:, :, c * 128:c * 128 + 384], reads=[hTd_b], writes=[hw_b])
            return hw_t, hw_b

        xtok = [sb([128, 2048], BF16, "xtok") for _ in range(1)]
        btok = [sb([128, 1024], BF16, "btok") for _ in range(1)]
        bT = [sb([128, 8, 128], BF16, "bT") for _ in range(1)]
        cT = [sb([128, 8, 128], BF16, "cT") for _ in range(1)]
        usb = [sb([128, 132], F32, "usb") for _ in range(2)]
        cacc = [sb([128, 128], F32, "cacc") for _ in range(2)]
        fT = [sb([128, 4, 128], BF16, "fT") for _ in range(2)]

        def xbc_chunk(c, need_c, hT_t, hw_b):
            x_t, x_b = xtok[0]
            bk_t, bk_b = btok[0]
            bT_t, bT_b = bT[0]
            cT_t, cT_b = cT[0]
            hbs = [hw_b]
            nblk = 8 if need_c else 6
            for blk in range(nblk):
                wv, wb = load_w(w_in[:, OFF_X + blk * 512:OFF_X + (blk + 1) * 512], 8, 512)
                f_t, f_b = fT[blk % 2]
                for j in range(4):
                    i = blk * 4 + j
                    p_t, p_b = psum()
                    for k in range(8):
                        mm(p_t[:, 0:132], wv[:, k, j * 128:(j + 1) * 128], hT_t[:, k, col - 2:col + 130],
                           k == 0, k == 7, hbs + [wb], [p_b])
                    u_t, u_b = usb[i % 2]
                    cp(ACT, u_t[:], p_t[:, 0:132], [p_b], [u_b])
                    a_t2, a_b2 = cacc[i % 2]
                    eng = DVE
                    ts(eng, a_t2[:], u_t[:, 0:128], cw_t[:, i * 5:i * 5 + 1], cb_t[:, i:i + 1], ALU.mult, ALU.add,
                       [u_b, cw_b, cb_b], [a_b2])
                    for kk in range(1, 5):
                        stt(eng, a_t2[:], u_t[:, kk:kk + 128], cw_t[:, i * 5 + kk:i * 5 + kk + 1], a_t2[:],
                            ALU.mult, ALU.add, [u_b, cw_b, a_b2], [a_b2])
                    if i < 16:
                        act(f_t[:, j, :], a_t2[:], AF.Silu, [a_b2], [f_b])
                    elif i < 24:
                        act(bT_t[:, i - 16, :], a_t2[:], AF.Silu, [a_b2], [bT_b])
                    else:
                        act(cT_t[:, i - 24, :], a_t2[:], AF.Silu, [a_b2], [cT_b])
                if blk < 6:
                    p_t, p_b = psum()
                    pv = p_t[:].bitcast(BF16)
                    for j in range(4):
                        src = f_t[:, j, :] if blk < 4 else bT_t[:, (blk - 4) * 4 + j, :]
                        srcb = f_b if blk < 4 else bT_b
                        tr(pv[:, j * 128:(j + 1) * 128], src, identb, [srcb, idb_b], [p_b])
                    if blk < 4:
                        cp(ACT, x_t[:, blk * 512:(blk + 1) * 512], pv[:, 0:512], [p_b], [x_b])
                    else:
                        cp(ACT, bk_t[:, (blk - 4) * 512:(blk - 3) * 512], pv[:, 0:512], [p_b], [bk_b])
            return x_t, x_b, bk_t, bk_b, bT_t, bT_b, cT_t, cT_b

        def bc32(ap2d):
            return ap2d.unsqueeze(2).to_broadcast([128, 32, 64])

        xw = [sb([128, 2048 if phase == 1 else 512], BF16, "xw") for _ in range(2)]
        finals = []

        if phase == 1:
            hf_t, hf_b = sb([128, 2048], F32, "hf")
            hb_t, hb_b = sb([128, 2048], F32, "hbs")
            hbb = [sb([128, 2048], BF16, "hbb") for _ in range(2)]
            suf_t, suf_b = sb([128, 32], F32, "suf")
            wf_t, wf_b = sb([128, 64], F32, "wf")
            fw.op(DVE, lambda e: e.memset(suf_t[:], 0.0), [], [suf_b])
            fw.op(DVE, lambda e: e.memset(hb_t[:], 0.0), [], [hb_b])
            fw.op(DVE, lambda e: e.memset(hf_t[:], 0.0), [], [hf_b])
            hbl_b = hbld_b
            for c in range(NCH - 1, -1, -1):
                x_t, x_b, bk_t, bk_b, _, _, _, _ = xbc_chunk(c, False, *load_window(c))
                nb_t, nb_b, ee_t, ee_b, et_t, etot_b = chunk_small(c)
                tt(DVE, wf_t[:, 0:32], tot_t[:, c, 0:32], suf_t[:], ALU.add, [tot_b, suf_b], [wf_b])
                cp(DVE, wf_t[:, 32:64], tot_t[:, c, 32:64], [tot_b], [wf_b])
                tt(DVE, wf_t[:], wf_t[:], cs2_t[:, c, :], ALU.subtract, [wf_b, cs2_b], [wf_b])
                act(wf_t[:], wf_t[:], AF.Exp, [wf_b], [wf_b])
                tt(DVE, wf_t[:], wf_t[:], dt_t[:, c, :], ALU.mult, [wf_b, dt_b], [wf_b])
                tt(DVE, suf_t[:], suf_t[:], tot_t[:, c, 0:32], ALU.add, [suf_b, tot_b], [suf_b])
                hq_t, hq_b = hbb[c % 2]
                cp(ACT, hq_t[:], hb_t[:], [hb_b], [hq_b])
                fw.dma(SP, hbl_d[c * 128:(c + 1) * 128, :], hq_t[:], reads=[hq_b], writes=[hbl_b], owner=hq_b)
                tt(DVE, hb_t[:].rearrange("p (h d) -> p h d", h=32), hb_t[:].rearrange("p (h d) -> p h d", h=32),
                   bc32(et_t[:, 32:64]), ALU.mult, [hb_b, etot_b], [hb_b])
                for d in range(2):
                    xw_t, xw_b = xw[d]
                    eng = DVE if d == 0 else POOL
                    tt(eng, xw_t[:].rearrange("p (h d) -> p h d", h=32), x_t[:].rearrange("p (h d) -> p h d", h=32),
                       bc32(wf_t[:, d * 32:(d + 1) * 32]), ALU.mult, [x_b, wf_b], [xw_b])
                    acc_t, acc_b = (hf_t, hf_b) if d == 0 else (hb_t, hb_b)
                    for qd in range(4):
                        p_t, p_b = psum()
                        for j in range(2):
                            g = qd * 2 + j
                            mm(p_t[:, j * 256:(j + 1) * 256], bk_t[:, g * 128:(g + 1) * 128],
                               xw_t[:, g * 256:(g + 1) * 256], True, True, [bk_b, xw_b], [p_b])
                        tt(DVE, acc_t[:, qd * 512:(qd + 1) * 512], acc_t[:, qd * 512:(qd + 1) * 512], p_t[:],
                           ALU.add, [acc_b, p_b], [acc_b])
            ob = [Buf("o1"), Buf("o2"), Buf("o3")]
            fw.dma(SP, sbnc[:, 0:2048], hf_t[:], reads=[hf_b], writes=[sbnc_b], owner=ob[0])
            fw.dma(SP, sbnc[:, 2048:4096], hb_t[:], reads=[hb_b], writes=[sbnc_b], owner=ob[0])
            td_t, td_b = sb([128, 64], F32, "td")
            cp(DVE, td_t[:], tot_t[:, 0, :], [tot_b], [td_b])
            for c in range(1, NCH):
                tt(DVE, td_t[:], td_t[:], tot_t[:, c, :], ALU.add, [td_b, tot_b], [td_b])
            fw.dma(SP, sbnc[:, 4096:4160], td_t[:], reads=[td_b], writes=[sbnc_b], owner=ob[0])
            return

        g3_t, g3_b = sb([128, 8], F32, "g3")
        fw.dma(SP, g3_t[:], g3d, writes=[g3_b], owner=sg)
        nw_t, nw_b = sb([128, 16], F32, "nw")
        fw.dma(SP, nw_t[:], nwd, writes=[nw_b], owner=sg)
        bg_t, bg_b = sb([128, 16], F32, "bg")
        fw.dma(SP, bg_t[:], bgd, writes=[bg_b], owner=sg)
        mk_t, mk_b = sb([128, 34], F32, "mk")
        fw.dma(SP, mk_t[:], mkd, writes=[mk_b], owner=sg)

        oh_t, oh_b = sb([33, 512], F32, "oh")
        fw.dma(SP, oh_t[:], ohd, writes=[oh_b], owner=sg)
        tb_t, tb_b = sb([33, 16], F32, "tb")
        fw.dma(SP, tb_t[:], tbd, writes=[tb_b], owner=sg)
        fw.close_group(sg, [g3_b, nw_b, bg_b, mk_b, oh_b, tb_b])
        p_t, p_b = psum()
        mm(p_t[0:16, :], tb_t[:], oh_t[:], True, True, [tb_b, oh_b], [p_b])
        vr_t, vr_b = sb([16, 512], F32, "vr")
        cp(DVE, vr_t[:], p_t[0:16, :], [p_b], [vr_b])
        vrd_b = Buf("vreld")
        fw.dma(SP, vreld.ap(), vr_t[:], reads=[vr_b], writes=[vrd_b])
        bias_t, bias_b = sb([128, 16, 384], BF16, "bias")
        hk = [sb([128, 384], F32, "hk") for _ in range(1)]
        for h in range(16):
            hk_t, hk_b = hk[0]
            fw.dma(SP, hk_t[:], bass.AP(vreld, h * 512, [[1, 128], [1, 384]]), reads=[vrd_b], writes=[hk_b])
            p_t, p_b = psum()
            mm(p_t[:, 0:384], antiid, hk_t[:], True, True, [cs_b, hk_b], [p_b])
            cp(ACT, bias_t[:, h, :], p_t[:, 0:384], [p_b], [bias_b])

        hin = []
        u_t, u_b = sb([128, 2048], F32, "u")
        st_t, st_b = u_t, u_b
        dc_t, dc_b = sb([128, 512], F32, "dc8")
        fw.dma(SP, dc_t[:].rearrange("p (r c) -> p r c", r=8), sgath.rearrange("(r p) c -> p r c", p=128)[:, :, 4096:4160], reads=[sgath_b], writes=[dc_b])
        act(dc_t[:], dc_t[:], AF.Exp, [dc_b], [dc_b])
        dd_t, dd_b = sb([128, 32], F32, "dd")
        for d in range(2):
            h_t, h_b = sb([128, 2048], BF16, "hin")
            fw.op(DVE, lambda e, h_t=h_t: e.memset(h_t[:], 0.0), [], [h_b])
            order = range(8) if d == 0 else range(7, -1, -1)
            for r in order:
                mcol = mk_t[:, 2 + d * 8 + r:3 + d * 8 + r]
                fw.dma(SP, st_t[:], sgath[r * 128:(r + 1) * 128, d * 2048:(d + 1) * 2048], reads=[sgath_b], writes=[st_b], owner=st_b)
                ts(DVE, dd_t[:], dc_t[:, r * 64 + d * 32:r * 64 + d * 32 + 32], -1.0, mcol, ALU.add, ALU.mult,
                   [dc_b, mk_b], [dd_b])
                ts(DVE, dd_t[:], dd_t[:], 1.0, None, ALU.add, ALU.bypass, [dd_b], [dd_b])
                hv = h_t[:].rearrange("p (h d) -> p h d", h=32)
                tt(DVE, hv, hv, bc32(dd_t[:]), ALU.mult, [h_b, dd_b], [h_b])
                stt(DVE, h_t[:], st_t[:], mcol, h_t[:], ALU.mult, ALU.add, [st_b, mk_b, h_b], [h_b])
            hin.append((h_t, h_b))

        ep_t, ep_b = sb([128, NCH, 32], F32, "epre")
        es_t, es_b = sb([128, NCH, 32], F32, "esuf")
        fw.op(DVE, lambda e: e.memset(ep_t[:, 0, :], 0.0), [], [ep_b])
        fw.op(DVE, lambda e: e.memset(es_t[:, NCH - 1, :], 0.0), [], [es_b])
        for c in range(1, NCH):
            tt(DVE, ep_t[:, c, :], ep_t[:, c - 1, :], tot_t[:, c - 1, 0:32], ALU.add, [ep_b, tot_b], [ep_b])
        for c in range(NCH - 2, -1, -1):
            tt(DVE, es_t[:, c, :], es_t[:, c + 1, :], tot_t[:, c + 1, 32:64], ALU.add, [es_b, tot_b], [es_b])
        act(ep_t[:], ep_t[:], AF.Exp, [ep_b], [ep_b])
        act(es_t[:], es_t[:], AF.Exp, [es_b], [es_b])
        wfc_t, wfc_b = sb([128, NCH, 32], F32, "wfc")
        tt(DVE, wfc_t[:], tot_t[:, :, 0:32], cs2_t[:, :, 0:32], ALU.subtract, [tot_b, cs2_b], [wfc_b])
        act(wfc_t[:], wfc_t[:], AF.Exp, [wfc_b], [wfc_b])
        tt(DVE, wfc_t[:], wfc_t[:], dt_t[:, :, 0:32], ALU.mult, [wfc_b, dt_b], [wfc_b])

        hfl_t, hfl_b = sb([128, 2048], F32, "hfl")
        fw.op(DVE, lambda e: e.memset(hfl_t[:], 0.0), [], [hfl_b])
        hflb_t, hflb_b = sb([128, 2048], BF16, "hflb")
        fw.op(DVE, lambda e: e.memset(hflb_t[:], 0.0), [], [hflb_b])
        hind = [sb([128, 2048], BF16, "hind") for _ in range(2)]
        hblc = [sb([128, 2048], BF16, "hblc") for _ in range(1)]
        sz_t, sz_b = sb([128, 2048], BF16, "sz")
        un_t, un_b = sb([128, 2048], BF16, "un")
        uT_t, uT_b = sb([128, 16, 128], BF16, "uT")
        cbm = [sb([128, 2, 256], F32, "cbm") for _ in range(1)]
        Tm = [sb([128, 512], F32, "Tm") for _ in range(1)]
        Em = [sb([128, 512], F32, "Em") for _ in range(1)]
        Mq = [sb([128, 4, 512], BF16, "Mq") for _ in range(1)]
        t1 = [sb([128, 512], F32, "t1") for _ in range(1)]
        t2 = [sb([128, 512], F32, "t2") for _ in range(1)]
        t3 = [sb([128, 512], F32, "t3") for _ in range(1)]
        qT_t, qT_b = sb([128, 8, 128], BF16, "qT")
        kT_t, kT_b = sb([128, 4, 384], BF16, "kT")
        v_t, v_b = sb([128, 3, 256], BF16, "v")
        lg = [sb([128, 384], F32, "lg") for _ in range(1)]
        pn = [sb([128, 384], BF16, "pn") for _ in range(1)]
        pT = [sb([128, 384], BF16, "pT") for _ in range(1)]
        sm = [sb([128, 8], F32, "sm") for _ in range(2)]
        oT_t, oT_b = sb([64, 16, 128], BF16, "oT")
        mxT_t, mxT_b = sb([128, 8, 128], BF16, "mxT")
        gsb = [sb([128, 128], F32, "gs") for _ in range(2)]
        mxa = [sb([128, 128], F32, "mxa") for _ in range(2)]
        xres = xin
        mo = [sb([128, D], F32, "mo") for _ in range(1)]
        x1_b = Buf("x1d")
        U_dir = [U_incl, U_rev]
        m_dir = [maskF, maskB]

        for c in range(NCH):
            hT_t, hw_b = load_window(c)
            hbc = [hw_b]
            x_t, x_b, bk_t, bk_b, bT_t, bT_b, cT_t, cT_b = xbc_chunk(c, True, hT_t, hw_b)
            nb_t, nb_b, ee_t, ee_b, et_t, etot_b = chunk_small(c)
            for zb in range(4):
                wv, wb = load_w(w_in[:, OFF_Z + zb * 512:OFF_Z + (zb + 1) * 512], 8, 512)
                p_t, p_b = psum()
                for k in range(8):
                    mm(p_t[:], hT_t[:, k, col:col + 128], wv[:, k, :], k == 0, k == 7, hbc + [wb], [p_b])
                act(sz_t[:, zb * 512:(zb + 1) * 512], p_t[:], AF.Silu, [p_b], [sz_b])
            hdf_t, hdf_b = hind[0]
            hdb_t, hdb_b = hind[1]
            tt(DVE, hdf_t[:].rearrange("p (h d) -> p h d", h=32), hin[0][0][:].rearrange("p (h d) -> p h d", h=32),
               bc32(ep_t[:, c, :]), ALU.mult, [hin[0][1], ep_b], [hdf_b])
            tt(DVE, hdb_t[:].rearrange("p (h d) -> p h d", h=32), hin[1][0][:].rearrange("p (h d) -> p h d", h=32),
               bc32(es_t[:, c, :]), ALU.mult, [hin[1][1], es_b], [hdb_b])
            hl_t, hl_b = hblc[0]
            fw.dma(SP, hl_t[:], hbl_d[c * 128:(c + 1) * 128, :], reads=[hbld_b], writes=[hl_b], owner=hl_b)
            for q in range(4):
                qs = slice(q * 512, (q + 1) * 512)
                hs = slice(q * 8, (q + 1) * 8)
                p_t, p_b = psum()
                for j in range(2):
                    g = 2 * q + j
                    mm(p_t[:, j * 128:(j + 1) * 128], bT_t[:, g, :], cT_t[:, g, :], True, True, [bT_b, cT_b], [p_b])
                cb_t2, cb_b2 = cbm[0]
                for d in range(2):
                    tt(DVE, cb_t2[:, d, :].rearrange("p (j l) -> p j l", j=2),
                       p_t[:, 0:256].rearrange("p (j l) -> p j l", j=2),
                       m_dir[d].unsqueeze(1).to_broadcast([128, 2, 128]), ALU.mult, [p_b, cs_b], [cb_b2])
                M_t, M_b = Mq[0]
                for d in range(2):
                    for j in range(2):
                        g = 2 * q + j
                        hd0 = d * 32 + g * 4
                        p2_t, p2_b = psum()
                        for r in range(4):
                            mm(p2_t[:, r * 128:(r + 1) * 128], adt_t[:, c, hd0 + r:hd0 + r + 1].to_broadcast([128, 128]),
                               U_dir[d], True, True, [adt_b, cs_b], [p2_b])
                        T_t, T_b = Tm[0]
                        tt(DVE, T_t[:].rearrange("p (r l) -> p r l", r=4), p2_t[:].rearrange("p (r l) -> p r l", r=4),
                           cs2_t[:, c, hd0:hd0 + 4].unsqueeze(2).to_broadcast([128, 4, 128]), ALU.min,
                           [p2_b, cs2_b], [T_b])
                        E_t, E_b = Em[0]
                        for r in range(4):
                            act(E_t[:, r * 128:(r + 1) * 128], T_t[:, r * 128:(r + 1) * 128], AF.Exp, [T_b, nb_b], [E_b],
                                bias=nb_t[:, hd0 + r:hd0 + r + 1])
                        tt(DVE, M_t[:, d * 2 + j, :].rearrange("p (r l) -> p r l", r=4),
                           E_t[:].rearrange("p (r l) -> p r l", r=4),
                           cb_t2[:, d, j * 128:(j + 1) * 128].unsqueeze(1).to_broadcast([128, 4, 128]), ALU.mult,
                           [E_b, cb_b2], [M_b])
                py_t, py_b = psum()
                for j in range(2):
                    for r in range(4):
                        hl = j * 4 + r
                        h = q * 8 + hl
                        mm(py_t[:, hl * 64:(hl + 1) * 64], M_t[:, j, r * 128:(r + 1) * 128], x_t[:, h * 64:(h + 1) * 64],
                           True, False, [M_b, x_b], [py_b])
                        mm(py_t[:, hl * 64:(hl + 1) * 64], M_t[:, 2 + j, r * 128:(r + 1) * 128],
                           x_t[:, h * 64:(h + 1) * 64], False, True, [M_b, x_b], [py_b])
                pf_t, pf_b = psum()
                pb_t, pb_b = psum()
                for j in range(2):
                    g = 2 * q + j
                    gs_ = slice(g * 256, (g + 1) * 256)
                    mm(pf_t[:, j * 256:(j + 1) * 256], cT_t[:, g, :], hflb_t[:, gs_], True, False, [cT_b, hflb_b], [pf_b])
                    mm(pf_t[:, j * 256:(j + 1) * 256], cT_t[:, g, :], hdf_t[:, gs_], False, True, [cT_b, hdf_b], [pf_b])
                    mm(pb_t[:, j * 256:(j + 1) * 256], cT_t[:, g, :], hl_t[:, gs_], True, False, [cT_b, hl_b], [pb_b])
                    mm(pb_t[:, j * 256:(j + 1) * 256], cT_t[:, g, :], hdb_t[:, gs_], False, True, [cT_b, hdb_b], [pb_b])
                a1, a1b = t1[0]
                a2, a2b = t2[0]
                a3, a3b = t3[0]

                def v8(ap):
                    return ap.rearrange("p (h d) -> p h d", h=8)

                def b8(ap2d):
                    return ap2d.unsqueeze(2).to_broadcast([128, 8, 64])

                tt(DVE, v8(a1[:]), v8(pf_t[:]), b8(ee_t[:, q * 8:(q + 1) * 8]), ALU.mult, [pf_b, ee_b], [a1b])
                tt(DVE, v8(a2[:]), v8(pb_t[:]), b8(ee_t[:, 32 + q * 8:32 + (q + 1) * 8]), ALU.mult, [pb_b, ee_b], [a2b])
                tt(DVE, v8(a3[:]), v8(x_t[:, qs]), b8(dsk[:, hs]), ALU.mult, [x_b, rp_b], [a3b])
                tt(DVE, a1[:], a1[:], a2[:], ALU.add, [a1b, a2b], [a1b])
                tt(DVE, a1[:], a1[:], a3[:], ALU.add, [a1b, a3b], [a1b])
                tt(DVE, a1[:], a1[:], py_t[:], ALU.add, [a1b, py_b], [a1b])
                tt(DVE, u_t[:, qs], a1[:], sz_t[:, qs], ALU.mult, [a1b, sz_b], [u_b])
                xw_t, xw_b = xw[q % 2]
                tt(DVE, v8(xw_t[:, 0:512]), v8(x_t[:, qs]), b8(wfc_t[:, c, hs]), ALU.mult, [x_b, wfc_b], [xw_b])
                ps_t, ps_b = psum()
                for j in range(2):
                    g = 2 * q + j
                    mm(ps_t[:, j * 256:(j + 1) * 256], bk_t[:, g * 128:(g + 1) * 128], xw_t[:, j * 256:(j + 1) * 256],
                       True, True, [bk_b, xw_b], [ps_b])
                tt(DVE, v8(hfl_t[:, qs]), v8(hfl_t[:, qs]), b8(et_t[:, hs]), ALU.mult, [hfl_b, etot_b], [hfl_b])
                tt(DVE, hfl_t[:, qs], hfl_t[:, qs], ps_t[:], ALU.add, [hfl_b, ps_b], [hfl_b])
                cp(ACT, hflb_t[:, qs], hfl_t[:, qs], [hfl_b], [hflb_b])
            if DEBUG:
                fw.dma(SP, dbg_u[c * 128:(c + 1) * 128, :], u_t[:], reads=[u_b], writes=[dbg_b], owner=u_b)
            r8_t, r8_b = rms_rstd(u_t[:], u_b, 8, 256)
            tt(DVE, un_t[:].rearrange("p (g d) -> p g d", g=8), u_t[:].rearrange("p (g d) -> p g d", g=8),
               r8_t.unsqueeze(2).to_broadcast([128, 8, 256]), ALU.mult, [u_b, r8_b], [un_b])
            for hf in range(2):
                p_t, p_b = psum()
                pv = p_t[:].bitcast(BF16)
                for k in range(8):
                    i = hf * 8 + k
                    tr(pv[:, k * 128:(k + 1) * 128], un_t[:, i * 128:(i + 1) * 128], identb, [un_b, idb_b], [p_b])
                tt(DVE, uT_t[:, hf * 8:(hf + 1) * 8, :], pv.rearrange("p (k c) -> p k c", k=8),
                   nw_t[:, hf * 8:(hf + 1) * 8].unsqueeze(2).to_broadcast([128, 8, 128]), ALU.mult, [p_b, nw_b], [uT_b])

            for blk in range(2):
                wv, wb = load_w(w_in[:, OFF_Q + blk * 512:OFF_Q + (blk + 1) * 512], 8, 512)
                for j in range(4):
                    p_t, p_b = psum()
                    for k in range(8):
                        mm(p_t[:, 0:128], wv[:, k, j * 128:(j + 1) * 128], hT_t[:, k, col:col + 128], k == 0, k == 7,
                           hbc + [wb], [p_b])
                    act(qT_t[:, blk * 4 + j, :], p_t[:, 0:128], AF.Copy, [p_b], [qT_b], scale=0.125)
            wv, wb = load_w(w_in[:, OFF_K:OFF_K + 512], 8, 512)
            hb3 = [hw_b]
            kd_t, kd_b = wpool[wi[0] % NW]
            wi[0] += 1
            kdv = kd_t[:, 0:8 * 512].rearrange("p (k c) -> p k c", k=8)
            for g in range(4):
                for half in range(2):
                    cp(ACT if half == 0 else DVE, kdv[:, :, g * 128 + half * 64:g * 128 + half * 64 + 64],
                       wv[:, :, g * 64:(g + 1) * 64], [wb], [kd_b])
            for g in range(4):
                p_t, p_b = psum()
                for k in range(8):
                    mm(p_t[:, 0:384], kdv[:, k, g * 128:(g + 1) * 128], hT_t[:, k, col - 128:col + 256], k == 0, k == 7,
                       hb3 + [kd_b], [p_b])
                cp(ACT, kT_t[:, g, :], p_t[:, 0:384], [p_b], [kT_b])
            for jt in range(3):
                p_t, p_b = psum()
                cc = col - 128 + jt * 128
                for k in range(8):
                    mm(p_t[:, 0:256], hT_t[:, k, cc:cc + 128], wv[:, k, 256:512], k == 0, k == 7, hb3 + [wb], [p_b])
                cp(ACT, v_t[:, jt, :], p_t[:, 0:256], [p_b], [v_b])
            for hq in range(16):
                g = hq // 4
                po = (hq % 2) * 64
                p_t, p_b = psum()
                mm(p_t[:, 0:384], qT_t[po:po + 64, hq // 2, :], kT_t[po:po + 64, g, :], True, True, [qT_b, kT_b], [p_b])
                l_t, l_b = lg[0]
                tt(DVE, l_t[:], p_t[:, 0:384], bias_t[:, hq, :], ALU.add, [p_b, bias_b], [l_b])
                if c == 0:
                    ts(DVE, l_t[:, 0:128], l_t[:, 0:128], mk_t[:, 0:1], None, ALU.add, ALU.bypass, [l_b, mk_b], [l_b])
                if c == NCH - 1:
                    ts(DVE, l_t[:, 256:384], l_t[:, 256:384], mk_t[:, 1:2], None, ALU.add, ALU.bypass, [l_b, mk_b], [l_b])
                s_t, s_b = sm[hq % 2]
                fw.op(DVE, lambda e, s_t=s_t, l_t=l_t: e.reduce_max(out=s_t[:, 0:1], in_=l_t[:], axis=AX.X), [l_b], [s_b])
                tt(DVE, s_t[:, 0:1], s_t[:, 0:1], sink[:, hq:hq + 1], ALU.max, [s_b, rp_b], [s_b])
                ts(DVE, s_t[:, 1:2], s_t[:, 0:1], -1.0, None, ALU.mult, ALU.bypass, [s_b], [s_b])
                act(l_t[:], l_t[:], AF.Exp, [l_b, s_b], [l_b, s_b], bias=s_t[:, 1:2], accum_out=s_t[:, 2:3])
                act(s_t[:, 3:4], sink[:, hq:hq + 1], AF.Exp, [rp_b, s_b], [s_b], bias=s_t[:, 1:2])
                tt(DVE, s_t[:, 4:5], s_t[:, 2:3], s_t[:, 3:4], ALU.add, [s_b], [s_b])
                recip(s_t[:, 5:6], s_t[:, 4:5], [s_b], [s_b])
                n_t, n_b = pn[0]
                ts(DVE, n_t[:], l_t[:], s_t[:, 5:6], None, ALU.mult, ALU.bypass, [l_b, s_b], [n_b])
                p2_t, p2_b = psum()
                pv = p2_t[:].bitcast(BF16)
                for jt in range(3):
                    tr(pv[:, jt * 128:(jt + 1) * 128], n_t[:, jt * 128:(jt + 1) * 128], identb, [n_b, idb_b], [p2_b])
                pt_t, pt_b = pT[0]
                cp(ACT, pt_t[:], pv[:, 0:384], [p2_b], [pt_b])
                p3_t, p3_b = psum()
                for jt in range(3):
                    mm(p3_t[0:64, 0:128], v_t[:, jt, g * 64:(g + 1) * 64], pt_t[:, jt * 128:(jt + 1) * 128], jt == 0,
                       jt == 2, [v_b, pt_b], [p3_b])
                cp(ACT, oT_t[:, hq, :], p3_t[0:64, 0:128], [p3_b], [oT_b])

            if DEBUG:
                fw.dma(SP, dbg_o[c * 64:(c + 1) * 64, :], oT_t[:].rearrange("p h c -> p (h c)"), reads=[oT_b], writes=[dbg_b], owner=oT_b)
            for blk in range(4):
                cs_ = slice(blk * 256, (blk + 1) * 256)
                wso_v, wso_b = load_w(w_so[:, cs_], 16, 256)
                wgs_v, wgs_b = load_w(w_in[:, OFF_G + blk * 256:OFF_G + (blk + 1) * 256], 8, 256)
                for j in range(2):
                    i = blk * 2 + j
                    js = slice(j * 128, (j + 1) * 128)
                    pa_t, pa_b = psum()
                    for k in range(16):
                        mm(pa_t[:, 0:128], wso_v[:, k, js], uT_t[:, k, :], k == 0, k == 15, [wso_b, uT_b], [pa_b])
                    pg_t, pg_b = psum()
                    for k in range(8):
                        mm(pg_t[:, 0:128], wgs_v[:, k, js], hT_t[:, k, col:col + 128], k == 0, k == 7, hbc + [wgs_b], [pg_b])
                    g_t, g_b = gsb[0]
                    act(g_t[:], pg_t[:, 0:128], AF.Sigmoid, [pg_b, bg_b], [g_b], bias=bg_t[:, i:i + 1])
                    m_t, m_b = mxa[j]
                    tt(DVE, m_t[:], g_t[:], pa_t[:, 0:128], ALU.mult, [g_b, pa_b], [m_b])
                wao_t, wao_b = wpool[wi[0] % NW]
                wi[0] += 1
                wao_v = wao_t[0:64, 0:16 * 256].rearrange("p (h c) -> p h c", h=16)
                fw.dma(POOL, wao_v, w_ao[:, cs_].rearrange("(h p) c -> p h c", p=64), writes=[wao_b])
                wga_v, wga_b = load_w(w_in[:, OFF_G + 1024 + blk * 256:OFF_G + 1024 + (blk + 1) * 256], 8, 256)
                for j in range(2):
                    i = blk * 2 + j
                    js = slice(j * 128, (j + 1) * 128)
                    m_t, m_b = mxa[j]
                    pc_t, pc_b = psum()
                    for h in range(16):
                        mm(pc_t[:, 0:128], wao_v[:, h, js], oT_t[:, h, :], h == 0, h == 15, [wao_b, oT_b], [pc_b])
                    pd_t, pd_b = psum()
                    for k in range(8):
                        mm(pd_t[:, 0:128], wga_v[:, k, js], hT_t[:, k, col:col + 128], k == 0, k == 7, hbc + [wga_b], [pd_b])
                    g2_t, g2_b = gsb[1]
                    act(g2_t[:], pd_t[:, 0:128], AF.Sigmoid, [pd_b, bg_b], [g2_b], bias=bg_t[:, 8 + i:9 + i])
                    tt(DVE, g2_t[:], g2_t[:], pc_t[:, 0:128], ALU.mult, [g2_b, pc_b], [g2_b])
                    tt(DVE, mxT_t[:, i, :], m_t[:], g2_t[:], ALU.add, [m_b, g2_b], [mxT_b])
            if DEBUG:
                fw.dma(SP, dbg_m[c * 128:(c + 1) * 128, :], mxT_t[:].rearrange("p h c -> p (h c)"), reads=[mxT_b], writes=[dbg_b], owner=mxT_b)
            mo_t, mo_b = mo[0]
            for half in range(2):
                wv, wb = load_w(w_o[:, half * 512:(half + 1) * 512], 8, 512)
                p_t, p_b = psum()
                for k in range(8):
                    mm(p_t[:], mxT_t[:, k, :], wv[:, k, :], k == 0, k == 7, [mxT_b, wb], [p_b])
                cp(ACT, mo_t[:, half * 512:(half + 1) * 512], p_t[:], [p_b], [mo_b])
            r_t, r_b = rms_rstd(mo_t[:], mo_b, 1, D)
            xr_t, xr_b = xres[c % 2]
            fw.dma(SP, xr_t[:], xh[128 + c * 128:256 + c * 128, :], reads=xh_rd, writes=[xr_b], owner=xr_b)
            stt(DVE, mo_t[:], mo_t[:], r_t[:, 0:1], pmix, ALU.mult, ALU.mult, [mo_b, r_b, rp_b], [mo_b])
            tt(DVE, xr_t[:], xr_t[:], mo_t[:], ALU.add, [xr_b, mo_b], [xr_b])
            fw.dma(SP, x1d[c * 128:(c + 1) * 128, :], xr_t[:], reads=[xr_b], writes=[x1_b], owner=xr_b)

        fw.barrier()
        scope[0].close()
        y_b = yout_b
        g3_t, g3_b = sb([128, 8], F32, "g3m")
        fw.dma(SP, g3_t[:], g3d, writes=[g3_b], owner=sg)
        pm_t, pm_b = sb([128, D], F32, "pmlp")
        fw.dma(SP, pm_t[:], rowp[:, 1200:2224].partition_broadcast(128).rearrange("p o c -> p (o c)"), writes=[pm_b], owner=sg)
        fw.close_group(sg, [g3_b, pm_b])
        xld, xst = Buf("xld"), Buf("xst")
        hb2.clear()
        hb2.append(sb([128, D], BF16, "hbm"))
        xg = [sb([128, D], F32, "xg") for _ in range(4)]
        mog = [sb([128, D], F32, "mog") for _ in range(4)]
        h2_t, h2_b = sb([128, 8, 512], BF16, "h2T")
        aT_t, aT_b = sb([128, 32, 512], BF16, "aT")
        rl = [sb([128, 512], F32, "rl") for _ in range(2)]
        for grp in range(NCH // 4):
            for t4 in range(4):
                c = grp * 4 + t4
                xr_t, xr_b = xg[t4]
                fw.dma(SP, xr_t[:], x1d[c * 128:(c + 1) * 128, :], reads=[x1_b], writes=[xr_b], owner=xld)
            fw.close_group(xld, [b for _, b in xg])
            for t4 in range(4):
                xr_t, xr_b = xg[t4]
                norm_transpose(xr_t[:], xr_b, g3_t[:], g3_b, h2_t[:, :, t4 * 128:(t4 + 1) * 128], h2_b)
            for blk in range(8):
                wv, wb = load_w(w1[:, blk * 512:(blk + 1) * 512], 8, 512)
                for j in range(4):
                    p_t, p_b = psum()
                    for k in range(8):
                        mm(p_t[:], wv[:, k, j * 128:(j + 1) * 128], h2_t[:, k, :], k == 0, k == 7, [wb, h2_b], [p_b])
                    r_t2, r_b2 = rl[j % 2]
                    act(r_t2[:], p_t[:], AF.Relu, [p_b], [r_b2])
                    tt(DVE, aT_t[:, blk * 4 + j, :], r_t2[:], r_t2[:], ALU.mult, [r_b2], [aT_b])
            for cb in range(2):
                acc = [psum() for _ in range(4)]
                for blk in range(8):
                    wv, wb = load_w(w2[blk * 512:(blk + 1) * 512, cb * 512:(cb + 1) * 512], 4, 512)
                    for f in range(4):
                        ff = blk * 4 + f
                        for t4 in range(4):
                            mm(acc[t4][0][:], aT_t[:, ff, t4 * 128:(t4 + 1) * 128], wv[:, f, :], ff == 0, ff == 31,
                               [aT_b, wb], [acc[t4][1]], inc=(ff == 31 or (f == 3 and t4 == 3)))
                for t4 in range(4):
                    cp(ACT, mog[t4][0][:, cb * 512:(cb + 1) * 512], acc[t4][0][:], [acc[t4][1]], [mog[t4][1]])
            for t4 in range(4):
                c = grp * 4 + t4
                xr_t, xr_b = xg[t4]
                mo_t, mo_b = mog[t4]
                r_t, r_b = rms_rstd(mo_t[:], mo_b, 1, D)
                stt(DVE, mo_t[:], mo_t[:], r_t[:, 0:1], pm_t[:], ALU.mult, ALU.mult, [mo_b, r_b, pm_b], [mo_b])
                tt(DVE, xr_t[:], xr_t[:], mo_t[:], ALU.add, [xr_b, mo_b], [xr_b])
                if l == 0:
                    fw.dma(SP, xh1[128 + c * 128:256 + c * 128, :], xr_t[:], reads=[xr_b], writes=[xh1_b], owner=xst)
                    if c == 0:
                        fw.dma(SP, xbn[0:128, :], xr_t[:], reads=[xr_b], writes=[xbn_b, xh1_b], owner=xst)
                    if c == NCH - 1:
                        fw.dma(SP, xbn[128:256, :], xr_t[:], reads=[xr_b], writes=[xbn_b, xh1_b], owner=xst)
                else:
                    fw.dma(SP, yout[c * 128:(c + 1) * 128, :], xr_t[:], reads=[xr_b], writes=[y_b], owner=xst)
        return

    for l in range(2):
        with ExitStack() as st:
            scope[0] = st
            emit(1, l)
            fw.barrier()
        scope[0] = None
        fw.collective("AllGather", ALU.bypass, sbnc, sgath, cdm_t[:], [sbnc_b], [sgath_b, cdm_b])
        fw.barrier()
        with ExitStack() as st:
            scope[0] = st
            emit(2, l)
            fw.barrier()
        scope[0] = None
        if l == 0:
            fw.collective("AllGather", ALU.bypass, xbn, xgt, cdm_t[:], [xbn_b], [xgt_b, cdm_b])
            fw.barrier()
            with ExitStack() as st:
                scope[0] = st
                mk_t, mk_b = sb([128, 34], F32, "mk")
                fw.dma(SP, mk_t[:], mkd, writes=[mk_b])
                acc = [sb([128, D], F32, "hacc") for _ in range(2)]
                ld = [sb([128, D], F32, "hld") for _ in range(2)]
                for side in range(2):
                    a_t, a_b = acc[side]
                    fw.op(DVE, lambda e, a_t=a_t: e.memset(a_t[:], 0.0), [], [a_b])
                    for r in range(8):
                        l_t, l_b = ld[r % 2]
                        row0 = r * 256 + (128 if side == 0 else 0)
                        fw.dma(SP, l_t[:], xgt[row0:row0 + 128, :], reads=[xgt_b], writes=[l_b], owner=l_b)
                        mcol = mk_t[:, 18 + side * 8 + r:19 + side * 8 + r]
                        fw.op(DVE, lambda e, a_t=a_t, l_t=l_t, mcol=mcol: e.scalar_tensor_tensor(
                            out=a_t[:], in0=l_t[:], scalar=mcol, in1=a_t[:], op0=ALU.mult, op1=ALU.add),
                            [l_b, mk_b, a_b], [a_b])
                    dst = xh1[0:128, :] if side == 0 else xh1[17 * 128:18 * 128, :]
                    fw.dma(SP, dst, a_t[:], reads=[a_b], writes=[xh1_b], owner=a_b)
                fw.barrier()
            scope[0] = None
    endt, endb = sb([128, 1], F32, "end")
    fw.op(POOL, lambda e: e.memset(endt[:], 0.0), [], [endb])
    fw.op(ACT, lambda e: e.copy(out=endt[:], in_=endt[:]), [endb], [endb])
    fw.op(DVE, lambda e: e.tensor_copy(out=endt[:], in_=endt[:]), [endb], [endb])
    fw.finish([yout_b, endb])
    return nc


def _t5_bucket(rel):
    nb = 16
    max_exact = 8
    ret = np.where(rel > 0, nb, 0)
    n = np.abs(rel)
    nf = np.maximum(n, 1).astype(np.float32)
    large = max_exact + (np.log(nf / max_exact) / math.log(128 / max_exact) * (nb - max_exact)).astype(np.int32)
    large = np.minimum(large, nb - 1)
    return ret + np.where(n < max_exact, n, large)


def _consts():
    I = np.eye(128, dtype=np.float32)
    tri = np.triu(np.ones((128, 128), np.float32))
    cst = np.concatenate([I, I[::-1], tri, tri.T, tri, tri.T, np.ones((128, 128), np.float32)], axis=1)
    rel = np.arange(511) - 255
    bk = _t5_bucket(rel)
    oh = np.zeros((33, 512), np.float32)
    for idx in range(511):
        if abs(rel[idx]) <= 128:
            oh[bk[idx], idx] = 1.0
        else:
            oh[32, idx] = 1.0
    oh[32, 511] = 1.0
    return np.ascontiguousarray(cst), oh


_PROG = []


def kernel(x, pre_mix_norm, w_in, b_gate, conv_w, conv_b, dt_bias, a_log, d_skip, ssd_norm,
           w_ssd_out, attn_sink, rel_bias_table, w_attn_out, w_o, post_mix_norm,
           pre_mlp_norm, w_mlp_in, w_mlp_out, post_mlp_norm):
    f = lambda a: np.ascontiguousarray(np.asarray(a), dtype=np.float32)
    x = f(x)
    cst, oh = _consts()
    tb = np.concatenate([f(rel_bias_table), np.full((1, 16), NEG, np.float32)], axis=0)
    p128 = lambda v, k: f(v).reshape(k, 128).T
    st2 = lambda fn: np.ascontiguousarray(np.concatenate([fn(0), fn(1)], axis=0))
    rowp = st2(lambda l: np.concatenate([f(dt_bias[l]).reshape(-1), f(a_log[l]).reshape(-1), f(d_skip[l]).reshape(-1),
                                         f(attn_sink[l]).reshape(-1), f(post_mix_norm[l]), f(post_mlp_norm[l])])[None, :])
    cw = st2(lambda l: f(conv_w[l])[:, 0, :].reshape(5, 32, 128).transpose(2, 1, 0).reshape(128, 160))
    common = {
        "w_in": f(w_in).reshape(2 * D, NIN), "cw": cw, "cb": st2(lambda l: p128(conv_b[l], 32)),
        "g1": st2(lambda l: p128(pre_mix_norm[l], 8)), "rowp": rowp, "cst": cst,
        "w_so": f(w_ssd_out).reshape(2 * 2048, D), "w_ao": f(w_attn_out).reshape(2 * D, D),
        "w_o": f(w_o).reshape(2 * D, D), "w1": f(w_mlp_in).reshape(2 * D, 4096),
        "w2": f(w_mlp_out).reshape(2 * 4096, D), "g3": st2(lambda l: p128(pre_mlp_norm[l], 8)),
        "nw": st2(lambda l: p128(ssd_norm[l], 16)), "bg": st2(lambda l: p128(b_gate[l], 16)), "oh": oh, "tb": tb,
    }
    xp = np.pad(x, ((0, 0), (128, 128), (0, 0)))
    ins = []
    for c in range(8):
        b, pos = c // 4, c % 4
        mk = np.zeros((128, 34), np.float32)
        mk[:, 0] = NEG if pos == 0 else 0.0
        mk[:, 1] = NEG if pos == 3 else 0.0
        for r in range(8):
            same = (r // 4 == b)
            mk[:, 2 + r] = 1.0 if (same and r < c) else 0.0
            mk[:, 10 + r] = 1.0 if (same and r > c) else 0.0
            mk[:, 18 + r] = 1.0 if (same and r == c - 1) else 0.0
            mk[:, 26 + r] = 1.0 if (same and r == c + 1) else 0.0
        ins.append(dict(common, xh=np.ascontiguousarray(xp[b, pos * 2048:pos * 2048 + 2304]), mk=mk))
    if not _PROG:
        _PROG.append(build_fused())
    res = run_bass_kernel_spmd(_PROG[0], ins, core_ids=list(range(8))).results
    out = np.zeros_like(x)
    for c in range(8):
        b, pos = c // 4, c % 4
        out[b, pos * 2048:(pos + 1) * 2048] = res[c]["y"]
    return out.astype(np.float32)
```
